# Optimizing a Trainium2 kernel written in Bass

```python
import math
import numpy as np
import jax
import jax.numpy as jnp
from jax import lax

D_MODEL = 1024
BATCH = 32
SEQ = 2048
DEPTH = 4

N_A_LAYERS = DEPTH // 2
N_B_LAYERS = DEPTH - N_A_LAYERS
DN_ALPHA = (2.0 * DEPTH) ** 0.25
DN_BETA = (8.0 * DEPTH) ** -0.25
LN_EPS = 1e-5
FFN_DIM = int(math.ceil(8 * D_MODEL / 3 / 256)) * 256
SGU_HIDDEN = 4 * D_MODEL
SGU_HALF = SGU_HIDDEN // 2
SGU_GROUPS = 8
SGU_GROUP_DIM = SGU_HALF // SGU_GROUPS
SGU_CHUNK = 128
N_HEADS = 16
N_KV_GROUPS = 4
HEADS_PER_GROUP = N_HEADS // N_KV_GROUPS
HEAD_DIM = D_MODEL // N_HEADS
CMP_LEN = 32
CMP_STRIDE = 16
PHI_HIDDEN = 256
SEL_LEN = 64
SEL_TOP_N = 8
N_LOCAL_BLOCKS = 2
WINDOW = 512
NSA_Q_BLOCK = 32
N_KV_SLOTS = 6
REL_BUCKETS = 32
REL_MAX_DIST = 128
NEG = -1e30

kernel_name = 'yoco_gmlp_nsa_macaron_deepnorm'


def layer_norm(x, g, b):
    xf = x.astype(jnp.float32)
    mu = jnp.mean(xf, axis=-1, keepdims=True)
    var = jnp.mean(jnp.square(xf - mu), axis=-1, keepdims=True)
    y = (xf - mu) * lax.rsqrt(var + LN_EPS)
    return (y * g.astype(jnp.float32) + b.astype(jnp.float32)).astype(x.dtype)


def post_norm(x, sub, g, b):
    return layer_norm(DN_ALPHA * x + sub, g, b)


def swiglu(x, w1, w3, w2):
    return (jax.nn.silu(x @ w1) * (x @ w3)) @ w2


def rel_bucket(dist):
    n = jnp.maximum(dist, 0)
    max_exact = REL_BUCKETS // 2
    nf = jnp.maximum(n, 1).astype(jnp.float32)
    large = max_exact + (jnp.log(nf / max_exact) / math.log(REL_MAX_DIST / max_exact)
                         * (REL_BUCKETS - max_exact)).astype(jnp.int32)
    return jnp.where(n < max_exact, n, jnp.minimum(large, REL_BUCKETS - 1))


def chunked_sgu_mixer(x, w_in, ln_g, ln_b, w_s, b_s, w_out):
    B, S, _ = x.shape
    z = jax.nn.gelu(x @ w_in, approximate=False)
    u, v = jnp.split(z, 2, axis=-1)
    v = layer_norm(v, ln_g, ln_b)
    n_chunks = S // SGU_CHUNK
    v = v.reshape(B, n_chunks, SGU_CHUNK, SGU_GROUPS, SGU_GROUP_DIM)
    causal = jnp.tril(jnp.ones((SGU_CHUNK, SGU_CHUNK), dtype=bool))
    w = jnp.where(causal[None], w_s, jnp.zeros_like(w_s))
    mixed = jnp.einsum('gts,bnsgc->bntgc', w, v) + b_s.T[None, None, :, :, None]
    return (u * mixed.reshape(B, S, SGU_HALF)) @ w_out


def compress(kv, pe, w1, b1, w2):
    B, S, G, Dh = kv.shape
    n_sub = CMP_LEN // CMP_STRIDE
    n_chunks = S // CMP_STRIDE
    n_cmp = n_chunks - n_sub + 1
    chunks = kv.reshape(B, n_chunks, CMP_STRIDE, G, Dh)
    blocks = jnp.concatenate([chunks[:, i:i + n_cmp] for i in range(n_sub)], axis=2)
    blocks = blocks + pe[None, None, :, None, :]
    flat = blocks.transpose(0, 1, 3, 2, 4).reshape(B, n_cmp, G, CMP_LEN * Dh)
    return jax.nn.gelu(flat @ w1 + b1, approximate=False) @ w2


def nsa_shared_kv(h, kv_w, cmp_pe, cmp_w1, cmp_b1, cmp_w2):
    B, S, _ = h.shape
    kv = (h @ kv_w).reshape(B, S, N_KV_SLOTS, N_KV_GROUPS, HEAD_DIM)
    k_cmp = compress(kv[:, :, 0], cmp_pe[0], cmp_w1[0], cmp_b1[0], cmp_w2[0])
    v_cmp = compress(kv[:, :, 1], cmp_pe[1], cmp_w1[1], cmp_b1[1], cmp_w2[1])
    n_sel = S // SEL_LEN
    k_sel = kv[:, :, 2].reshape(B, n_sel, SEL_LEN, N_KV_GROUPS, HEAD_DIM).transpose(0, 3, 1, 2, 4)
    v_sel = kv[:, :, 3].reshape(B, n_sel, SEL_LEN, N_KV_GROUPS, HEAD_DIM).transpose(0, 3, 1, 2, 4)
    pad = ((0, 0), (WINDOW, 0), (0, 0), (0, 0))
    k_win = jnp.pad(kv[:, :, 4], pad)
    v_win = jnp.pad(kv[:, :, 5], pad)
    return (k_cmp, v_cmp, k_sel, v_sel, k_win, v_win)


def masked_softmax(s, mask):
    p = jax.nn.softmax(jnp.where(mask, s, NEG), axis=-1)
    return jnp.where(mask, p, 0.0)


def cmp_to_sel_overlap(n_cmp, n_sel):
    i = np.arange(n_cmp)[:, None]
    j = np.arange(n_sel)[None, :]
    ov = (i * CMP_STRIDE < (j + 1) * SEL_LEN) & (i * CMP_STRIDE + CMP_LEN > j * SEL_LEN)
    return ov.astype(np.float32)


def nsa_mixer(h, shared, w_qg, w_o, rel_table):
    k_cmp, v_cmp, k_sel, v_sel, k_win, v_win = shared
    B, S, _ = h.shape
    G, HG, Dh, QB = N_KV_GROUPS, HEADS_PER_GROUP, HEAD_DIM, NSA_Q_BLOCK
    qg = h @ w_qg
    q = qg[..., :N_HEADS * Dh].reshape(B, S, G, HG, Dh) * (Dh ** -0.5)
    gates = jax.nn.sigmoid(qg[..., N_HEADS * Dh:]).reshape(B, S, G, HG, 3)
    n_qb = S // QB
    n_cmp = k_cmp.shape[1]
    n_sel = k_sel.shape[2]
    n_top = min(SEL_TOP_N, n_sel)
    overlap = jnp.asarray(cmp_to_sel_overlap(n_cmp, n_sel))
    cmp_end = jnp.arange(n_cmp, dtype=jnp.int32) * CMP_STRIDE + (CMP_LEN - 1)
    sel_ids = jnp.arange(n_sel, dtype=jnp.int32)
    b_idx = jnp.arange(B)[:, None, None]
    g_idx = jnp.arange(G)[None, :, None]
    g_idx4 = jnp.arange(G)[None, :, None, None]
    tbl_g = rel_table.reshape(REL_BUCKETS, G, HG)

    def head_bias(dist):
        return rel_table[rel_bucket(dist)].reshape(dist.shape + (G, HG)).transpose(2, 3, 0, 1).astype(jnp.float32)

    def block(args):
        q_b, g_b, q0 = args
        t = q0 + jnp.arange(QB, dtype=jnp.int32)
        dist_c = t[:, None] - cmp_end[None, :]
        mask_c = dist_c >= 0
        s_c = jnp.einsum('bqghd,bngd->bghqn', q_b, k_cmp).astype(jnp.float32) + head_bias(dist_c)
        p_c = masked_softmax(s_c, mask_c)
        o_c = jnp.einsum('bghqn,bngd->bqghd', p_c.astype(v_cmp.dtype), v_cmp)
        imp = jnp.einsum('bghqn,nj->bgqj', p_c, overlap)
        cur = t // SEL_LEN
        valid = sel_ids[None, :] <= cur[:, None]
        forced = valid & ((sel_ids[None, :] == 0) | (sel_ids[None, :] > cur[:, None] - N_LOCAL_BLOCKS))
        imp = jnp.where(forced, -NEG, jnp.where(valid, imp, NEG))
        _, idx = lax.top_k(imp, n_top)
        flat = idx.reshape(B, G, QB * n_top)
        k_g = k_sel[b_idx, g_idx, flat].reshape(B, G, QB, n_top * SEL_LEN, Dh)
        v_g = v_sel[b_idx, g_idx, flat].reshape(B, G, QB, n_top * SEL_LEN, Dh)
        pos = (idx[..., None] * SEL_LEN + jnp.arange(SEL_LEN, dtype=jnp.int32)).reshape(B, G, QB, n_top * SEL_LEN)
        dist_s = t[None, None, :, None] - pos
        mask_s = (dist_s >= 0)[:, :, None]
        bias_s = tbl_g[rel_bucket(dist_s), g_idx4].transpose(0, 1, 4, 2, 3).astype(jnp.float32)
        s_s = jnp.einsum('bqghd,bgqkd->bghqk', q_b, k_g).astype(jnp.float32) + bias_s
        p_s = masked_softmax(s_s, mask_s)
        o_s = jnp.einsum('bghqk,bgqkd->bqghd', p_s.astype(v_g.dtype), v_g)
        k_w = lax.dynamic_slice_in_dim(k_win, q0, WINDOW + QB, axis=1)
        v_w = lax.dynamic_slice_in_dim(v_win, q0, WINDOW + QB, axis=1)
        pos_w = q0 - WINDOW + jnp.arange(WINDOW + QB, dtype=jnp.int32)
        dist_w = t[:, None] - pos_w[None, :]
        mask_w = (dist_w >= 0) & (dist_w < WINDOW) & (pos_w[None, :] >= 0)
        s_w = jnp.einsum('bqghd,bkgd->bghqk', q_b, k_w).astype(jnp.float32) + head_bias(dist_w)
        p_w = masked_softmax(s_w, mask_w)
        o_w = jnp.einsum('bghqk,bkgd->bqghd', p_w.astype(v_w.dtype), v_w)
        return g_b[..., 0:1] * o_c + g_b[..., 1:2] * o_s + g_b[..., 2:3] * o_w

    q_blocks = q.reshape(B, n_qb, QB, G, HG, Dh).swapaxes(0, 1)
    g_blocks = gates.reshape(B, n_qb, QB, G, HG, 3).swapaxes(0, 1)
    starts = jnp.arange(n_qb, dtype=jnp.int32) * QB
    out = lax.map(block, (q_blocks, g_blocks, starts))
    out = out.swapaxes(0, 1).reshape(B, S, N_HEADS * Dh)
    return out @ w_o


def setup_inputs(seed: int = 0) -> dict:
    key = jax.random.key(seed)
    ks = jax.random.split(key, 24)
    f32 = jnp.float32

    def nrm(k, shape, scale):
        return jax.random.normal(k, shape, f32) * scale

    kv_scale = jnp.array([1.0, DN_BETA, 1.0, DN_BETA, 1.0, DN_BETA], f32)[None, :, None]
    kv_w = (nrm(ks[13], (D_MODEL, N_KV_SLOTS, N_KV_GROUPS * HEAD_DIM), D_MODEL ** -0.5) * kv_scale)
    return {
        'x': nrm(ks[0], (BATCH, SEQ, D_MODEL), 1.0),
        'rel_table': nrm(ks[1], (REL_BUCKETS, N_HEADS), 0.5),
        'ln_g': 1.0 + nrm(ks[2], (DEPTH, 3, D_MODEL), 0.05),
        'ln_b': nrm(ks[3], (DEPTH, 3, D_MODEL), 0.02),
        'ffn_w1': nrm(ks[4], (DEPTH, 2, D_MODEL, FFN_DIM), D_MODEL ** -0.5),
        'ffn_w3': nrm(ks[5], (DEPTH, 2, D_MODEL, FFN_DIM), D_MODEL ** -0.5),
        'ffn_w2': nrm(ks[6], (DEPTH, 2, FFN_DIM, D_MODEL), DN_BETA * FFN_DIM ** -0.5),
        'sgu_w_in': nrm(ks[7], (N_A_LAYERS, D_MODEL, SGU_HIDDEN), D_MODEL ** -0.5),
        'sgu_ln_g': 1.0 + nrm(ks[8], (N_A_LAYERS, SGU_HALF), 0.05),
        'sgu_ln_b': nrm(ks[9], (N_A_LAYERS, SGU_HALF), 0.02),
        'sgu_w_s': nrm(ks[10], (N_A_LAYERS, SGU_GROUPS, SGU_CHUNK, SGU_CHUNK), 0.05),
        'sgu_b_s': 1.0 + nrm(ks[11], (N_A_LAYERS, SGU_GROUPS, SGU_CHUNK), 0.05),
        'sgu_w_out': nrm(ks[12], (N_A_LAYERS, SGU_HALF, D_MODEL), DN_BETA * SGU_HALF ** -0.5),
        'kv_w': kv_w.reshape(D_MODEL, N_KV_SLOTS * N_KV_GROUPS * HEAD_DIM),
        'cmp_pe': nrm(ks[14], (2, CMP_LEN, HEAD_DIM), 0.5),
        'cmp_w1': nrm(ks[15], (2, CMP_LEN * HEAD_DIM, PHI_HIDDEN), (CMP_LEN * HEAD_DIM) ** -0.5),
        'cmp_b1': nrm(ks[16], (2, PHI_HIDDEN), 0.02),
        'cmp_w2': nrm(ks[17], (2, PHI_HIDDEN, HEAD_DIM), PHI_HIDDEN ** -0.5),
        'nsa_w_qg': nrm(ks[18], (N_B_LAYERS, D_MODEL, N_HEADS * HEAD_DIM + 3 * N_HEADS), D_MODEL ** -0.5),
        'nsa_w_o': nrm(ks[19], (N_B_LAYERS, N_HEADS * HEAD_DIM, D_MODEL), DN_BETA * (N_HEADS * HEAD_DIM) ** -0.5),
    }


def reference(x, rel_table, ln_g, ln_b, ffn_w1, ffn_w3, ffn_w2, sgu_w_in, sgu_ln_g, sgu_ln_b, sgu_w_s,
              sgu_b_s, sgu_w_out, kv_w, cmp_pe, cmp_w1, cmp_b1, cmp_w2, nsa_w_qg, nsa_w_o):
    h = x
    shared = None
    for layer in range(DEPTH):
        if layer == N_A_LAYERS:
            shared = nsa_shared_kv(h, kv_w, cmp_pe, cmp_w1, cmp_b1, cmp_w2)
        h = post_norm(h, 0.5 * swiglu(h, ffn_w1[layer, 0], ffn_w3[layer, 0], ffn_w2[layer, 0]),
                      ln_g[layer, 0], ln_b[layer, 0])
        if layer < N_A_LAYERS:
            a = layer
            mix = chunked_sgu_mixer(h, sgu_w_in[a], sgu_ln_g[a], sgu_ln_b[a], sgu_w_s[a], sgu_b_s[a], sgu_w_out[a])
        else:
            bl = layer - N_A_LAYERS
            mix = nsa_mixer(h, shared, nsa_w_qg[bl], nsa_w_o[bl], rel_table)
        h = post_norm(h, mix, ln_g[layer, 1], ln_b[layer, 1])
        h = post_norm(h, 0.5 * swiglu(h, ffn_w1[layer, 1], ffn_w3[layer, 1], ffn_w2[layer, 1]),
                      ln_g[layer, 2], ln_b[layer, 2])
    return h
```

```python
import math
from contextlib import ExitStack
import numpy as np
import concourse.bass as bass
import concourse.mybir as mybir
from concourse.bass_utils import run_bass_kernel_spmd

F32 = mybir.dt.float32
BF16 = mybir.dt.bfloat16
AF = mybir.ActivationFunctionType
ALU = mybir.AluOpType
AX = mybir.AxisListType

D = 1024
S = 2048
DEPTH = 4
FF = 2816
NFC = 22
CH = 512
NCH = S // CH
NCORES = 8
BPC = 4
ALPHA = (2.0 * DEPTH) ** 0.25
EPS_LN = 1e-5
NEGM = -32768.0
BIG = 1e30
SB_BASE = 16512
SB_LIMIT = 229344
CONV_W = 2048

ENGS = ("pe", "act", "dve", "pool", "sp")
CENGS = ("pe", "act", "dve", "pool")


class Res:
    __slots__ = ("w", "rs")

    def __init__(self):
        self.w = None
        self.rs = {}


class Prog:
    def __init__(self, nc):
        self.nc = nc
        self.ops = {e: [] for e in ENGS}
        self.cnt = {}
        self.waited = {e: {} for e in ENGS}

    def op(self, eng, fn, reads=(), writes=(), dma=None):
        deps = {}
        for r in reads:
            if r.w is not None:
                k, v = r.w
                if deps.get(k, 0) < v:
                    deps[k] = v
        for w in writes:
            if w.w is not None:
                k, v = w.w
                if deps.get(k, 0) < v:
                    deps[k] = v
            for k, v in w.rs.items():
                if deps.get(k, 0) < v:
                    deps[k] = v
        waits = []
        wd = self.waited[eng]
        for k, v in deps.items():
            if k == eng and eng == "pe":
                continue
            if wd.get(k, 0) >= v:
                continue
            wd[k] = v
            waits.append((k, v))
        if dma is not None:
            key, inc = dma, 16
        else:
            key, inc = eng, 1
        v = self.cnt.get(key, 0) + inc
        self.cnt[key] = v
        self.ops[eng].append((waits, fn, key, inc))
        for r in reads:
            if r.rs.get(key, 0) < v:
                r.rs[key] = v
        for w in writes:
            w.w = (key, v)
            w.rs = {}

    def barrier(self, with_sp=False):
        snap = dict(self.cnt)
        for e in (ENGS if with_sp else CENGS):
            waits = []
            wd = self.waited[e]
            for k, v in snap.items():
                if k == e and e == "pe":
                    continue
                if wd.get(k, 0) >= v:
                    continue
                wd[k] = v
                waits.append((k, v))
            if waits:
                self.ops[e].append((waits, None, None, 0))

    def final_wait(self, eng):
        snap = dict(self.cnt)
        self.ops[eng].append((list(snap.items()), None, None, 0))

    def replay(self, stack):
        nc = self.nc
        sems = {}
        for k in self.cnt.keys():
            sems[k] = stack.enter_context(nc.semaphore("s_" + str(k)))
        block = stack.enter_context(nc.Block())

        def run(engine, lst):
            for waits, fn, sk, inc in lst:
                for k, v in waits:
                    engine.wait_ge(sems[k], v)
                if fn is not None:
                    fn(engine).then_inc(sems[sk], inc)

        @block.tensor
        def _(e):
            run(e, self.ops["pe"])

        @block.scalar
        def _(e):
            run(e, self.ops["act"])

        @block.vector
        def _(e):
            run(e, self.ops["dve"])

        @block.gpsimd
        def _(e):
            run(e, self.ops["pool"])

        @block.sync
        def _(e):
            run(e, self.ops["sp"])


def _rel_bucket(dist):
    n = np.maximum(dist, 0)
    nf = np.maximum(n, 1).astype(np.float32)
    large = 16 + (np.log(nf / np.float32(16)) / np.float32(math.log(128 / 16)) * np.float32(16)).astype(np.int32)
    return np.where(n < 16, n, np.minimum(large, 31)).astype(np.int64)


class WCat:
    def __init__(self):
        self.parts = []
        self.off = {}
        self.n = 0

    def add(self, name, arr):
        arr = np.ascontiguousarray(arr, dtype=np.float32)
        assert arr.shape[1] == 128, (name, arr.shape)
        x = int(np.prod(arr.shape[2:]))
        self.off[name] = (self.n, x, arr.shape[0])
        self.parts.append(arr.reshape(-1))
        self.n += arr.size

    def finish(self):
        blk = 128 * CONV_W
        pad = (-self.n) % blk
        if pad:
            self.parts.append(np.zeros(pad, np.float32))
            self.n += pad
        return np.concatenate(self.parts)


def _wlayout(inp):
    wc = WCat()
    for l in range(DEPTH):
        for j in range(2):
            w1 = inp["ffn_w1"][l, j].reshape(8, 128, 11, 256).transpose(2, 1, 0, 3)
            w3 = inp["ffn_w3"][l, j].reshape(8, 128, 11, 256).transpose(2, 1, 0, 3)
            w13 = np.concatenate([w1.reshape(11, 128, 2048), w3.reshape(11, 128, 2048)], axis=2)
            wc.add("w13_%d_%d" % (l, j), w13)
            wc.add("w2_%d_%d" % (l, j), inp["ffn_w2"][l, j].reshape(22, 128, 8, 128).transpose(2, 1, 0, 3))
    for a in range(2):
        wc.add("win_%d" % a, inp["sgu_w_in"][a].reshape(8, 128, 16, 256).transpose(2, 1, 0, 3))
        wc.add("wout_%d" % a, inp["sgu_w_out"][a].reshape(16, 128, 8, 128).transpose(2, 1, 0, 3))
        wc.add("wsT_%d" % a, inp["sgu_w_s"][a].transpose(2, 0, 1)[None])
    kv = inp["kv_w"].reshape(D, 6, 4, 64)
    kk = np.stack([kv[:, 2], kv[:, 4]], axis=1)
    kk = np.repeat(kk[:, :, :, None, :], 2, axis=3).reshape(D, 1024)
    wc.add("kk", kk.reshape(8, 128, 4, 256).transpose(2, 1, 0, 3))
    kvv = np.stack([kv[:, 3], kv[:, 5]], axis=1).reshape(D, 512)
    wc.add("kvv", kvv.reshape(8, 128, 512).transpose(1, 0, 2)[None])
    kvc = np.stack([kv[:, 0], kv[:, 1]], axis=1).reshape(D, 512)
    wc.add("kvc", kvc.reshape(8, 128, 512).transpose(1, 0, 2)[None])
    w1c = inp["cmp_w1"].reshape(2, 32, 64, 2, 128).transpose(0, 3, 2, 1, 4)
    zz = np.zeros_like(w1c)
    w1c = np.stack([np.concatenate([w1c, zz], axis=2), np.concatenate([zz, w1c], axis=2)], axis=2)
    wc.add("w1c", w1c.reshape(8, 128, 32, 128))
    w2k = inp["cmp_w2"][0].reshape(2, 128, 64).transpose(1, 0, 2)
    wc.add("w2ck", np.concatenate([w2k, w2k], axis=2)[None])
    wc.add("w2cv", inp["cmp_w2"][1].reshape(2, 128, 64).transpose(1, 0, 2)[None])
    peT = inp["cmp_pe"].transpose(2, 0, 1)
    wc.add("peT", np.concatenate([peT, peT], axis=0)[None])
    for bl in range(2):
        wq = inp["nsa_w_qg"][bl][:, :1024]
        wc.add("wq_%d" % bl, wq.reshape(8, 128, 4, 256).transpose(2, 1, 0, 3))
        wg = inp["nsa_w_qg"][bl][:, 1024:]
        wc.add("wg_%d" % bl, wg.reshape(8, 128, 48).transpose(1, 0, 2)[None])
        wc.add("wo_%d" % bl, inp["nsa_w_o"][bl].reshape(8, 128, 4, 256).transpose(2, 1, 0, 3))
    ii = np.arange(128)
    wc.add("ident", np.eye(128, dtype=np.float32)[None])
    wc.add("ones", np.ones((1, 128, 128), np.float32))
    wc.add("tril", (ii[None, :] >= ii[:, None]).astype(np.float32)[None])
    wc.add("wmask", np.where(ii[None, :] >= ii[:, None], NEGM, 0.0).astype(np.float32)[None])
    em = np.zeros((128, 2048), np.float32)
    for j in range(32):
        em[j, 64 * j:64 * j + 64] = 1.0
    wc.add("emast", em[None])
    ov = np.zeros((128, 32), np.float32)
    for m in range(1, 128):
        for j in range(32):
            if 4 * j <= m <= 4 * j + 4:
                ov[m, j] = 1.0
    wc.add("ov", ov[None])
    return wc


def _small_params(inp):
    cols = {}
    parts = []
    n = 0

    def add(name, a):
        nonlocal n
        a = np.ascontiguousarray(a, np.float32).reshape(128, -1)
        cols[name] = (n, a.shape[1])
        parts.append(a)
        n += a.shape[1]

    add("lnG", inp["ln_g"].reshape(12, 8, 128).transpose(2, 0, 1))
    add("lnB", inp["ln_b"].reshape(12, 8, 128).transpose(2, 0, 1))
    add("sgG", inp["sgu_ln_g"].reshape(2, 16, 128).transpose(2, 0, 1))
    add("sgB", inp["sgu_ln_b"].reshape(2, 16, 128).transpose(2, 0, 1))
    add("b1T", inp["cmp_b1"].reshape(2, 2, 128).transpose(2, 0, 1))
    ii = np.arange(128)
    add("rv0", (ii >= 31).astype(np.float32)[:, None])
    ci = (ii >= 64).astype(np.int64)[:, None]
    x = np.arange(-30, 32)[None, :]
    forced = (x == ci - 1) | (x == ci)
    invalid = x > ci
    vm = (~forced & ~invalid).astype(np.float32)
    fb = np.where(forced, BIG, np.where(invalid, -BIG, 0.0)).astype(np.float32)
    add("VM", vm)
    add("FB", fb)
    pcat = np.concatenate(parts, axis=1)
    bcat = np.concatenate([inp["rel_table"].reshape(-1), inp["sgu_b_s"].reshape(-1)]).astype(np.float32)[None]
    j = ii[:, None]
    y = np.arange(256)[None, :]
    dT = y - j
    bT = _rel_bucket(dT)
    oht = np.stack([((bT == b) & (dT >= 0)).astype(np.float32) for b in range(31)], axis=1)
    negt = np.where(dT < 0, NEGM, 0.0).astype(np.float32)
    xx = np.arange(-120, 32)[None, :]
    dC = j - 16 * xx - 15
    bC = _rel_bucket(dC)
    ohc = np.stack([((bC == b) & (dC >= 0)).astype(np.float32) for b in range(31)], axis=1)
    negc = np.where(dC < 0, NEGM, 0.0).astype(np.float32)
    ohT = np.concatenate([oht.reshape(128, -1), negt], axis=1)
    ohC = np.concatenate([ohc.reshape(128, -1), negc], axis=1)
    return pcat, cols, bcat, np.ascontiguousarray(ohT), np.ascontiguousarray(ohC)


DTSIZE = {F32: 4, BF16: 2}


def build_nc(woff, wtotal, pcols, npcat, cfg):
    nseq = cfg.get("nseq", BPC)
    nchunk = cfg.get("nchunk", NCH)
    layers = cfg.get("layers", list(range(DEPTH)))
    do_mixer = cfg.get("mixer", True)
    do_ffn = cfg.get("ffn", True)
    stage = cfg.get("nsa_stage", 9)

    nc = bass.Bass("TRN2", target_bir_lowering=False)
    xT = nc.dram_tensor("xT", [BPC, D, S], F32, kind="ExternalInput")
    wcat = nc.dram_tensor("wcat", [wtotal], F32, kind="ExternalInput")
    pcat_d = nc.dram_tensor("pcat", [128, npcat], F32, kind="ExternalInput")
    bcat_d = nc.dram_tensor("bcat", [1, 2560], F32, kind="ExternalInput")
    ohT_d = nc.dram_tensor("ohT", [128, 8192], F32, kind="ExternalInput")
    ohC_d = nc.dram_tensor("ohC", [128, 4864], F32, kind="ExternalInput")
    oT = nc.dram_tensor("oT", [BPC, D, S], F32, kind="ExternalOutput")
    wbf = nc.dram_tensor("wbf", [wtotal], BF16)

    P = Prog(nc)
    uid = [0]
    ptr = [SB_BASE]

    def sb(shape, dt, name="t"):
        uid[0] += 1
        size = int(np.prod(shape[1:])) * DTSIZE[dt]
        size = (size + 31) // 32 * 32
        off = ptr[0]
        assert off + size <= SB_LIMIT, ("SBUF overflow", name, off, size)
        ptr[0] = off + size
        return nc.alloc_sbuf_tensor_at("%s_%d" % (name, uid[0]), list(shape), dt, offset=off)

    def wtile(name, idx):
        off, x, nt = woff[name]
        assert idx < nt
        return bass.AP(wbf, off + idx * 128 * x, [[x, 128], [1, x]])

    with ExitStack() as st:
        banks = [st.enter_context(nc.psum_tensor("bank%d" % i, [128, 512], F32)) for i in range(8)]
        Rb = [Res() for _ in range(8)]

        pc_t = sb([128, npcat], F32, "pcat"); R_pc = Res()
        bc_t = sb([128, 2560], F32, "bcat"); R_bc = Res()
        c_ident = sb([128, 128], BF16, "ident")
        c_ones = sb([128, 128], BF16, "ones")
        c_tril = sb([128, 128], BF16, "tril")
        c_wmask = sb([128, 128], BF16, "wmask")
        c_emast = sb([128, 2048], BF16, "emast")
        c_ov = sb([128, 32], BF16, "ov")
        c_w2ck = sb([128, 2, 128], BF16, "w2ck")
        c_w2cv = sb([128, 2, 64], BF16, "w2cv")
        c_peT = sb([128, 2, 32], BF16, "peT")
        c_mhalf = sb([128, 512], F32, "mhalf")
        tabd = sb([128, 512], F32, "tabd")
        cbias = sb([128, 4], F32, "cbias")
        R_const = Res()
        MT = sb([128, 16, 256], BF16, "MT")
        MC = sb([128, 16, 152], BF16, "MC")
        R_mast = Res()
        kTs = sb([128, 8, S], BF16, "kTs"); R_kTs = Res()
        Vaug = sb([128, 16, 8, 65], BF16, "Vaug"); R_V = Res()
        kcT = sb([128, 4, 128], BF16, "kcT"); R_kc = Res()
        hidv = sb([128, 2, 4, 128], BF16, "hidv"); R_hidv = Res()
        vcmp = sb([128, 4, 64], BF16, "vcmp"); R_vc = Res()
        carry = sb([128, 4, 16], BF16, "carry"); R_carry = Res()
        h32 = sb([128, 8, CH], F32, "h32"); R_h = [Res() for _ in range(8)]
        hb = sb([128, 8, CH], BF16, "hb"); R_hb = [Res() for _ in range(8)]
        wA = [sb([128, 4096], BF16, "wA") for _ in range(2)]; R_wA = [Res(), Res()]
        wB = [sb([128, 22 * 128], BF16, "wB") for _ in range(2)]; R_wB = [Res(), Res()]
        WT = sb([128, 8, 128], BF16, "WT"); R_WT = Res()
        wgt = sb([128, 8, 48], BF16, "wgt"); R_wgt = Res()
        local_base = ptr[0]
        cntA = [0]
        cntB = [0]

        def lnp(name, idx):
            o, n = pcols[name]
            return pc_t[:, o + idx:o + idx + 1]

        def pslice(name):
            o, n = pcols[name]
            return pc_t[:, o:o + n]

        def reset_local():
            ptr[0] = local_base

        def mm(out_ap, pairs, reads, writes, start=True, stop=True, skip=False):
            pairs = list(pairs)

            def fn(e):
                n = len(pairs)
                ins = None
                for i, (l, r) in enumerate(pairs):
                    ins = e.matmul(out_ap, l, r, start=(start and i == 0), stop=(stop and i == n - 1),
                                   skip_group_check=skip)
                return ins
            P.op("pe", fn, reads, writes)

        def mm_multi(items, reads, writes):
            items = list(items)

            def fn(e):
                ins = None
                for (o, l, r, s0, s1, sk) in items:
                    ins = e.matmul(o, l, r, start=s0, stop=s1, skip_group_check=sk)
                return ins
            P.op("pe", fn, reads, writes)

        def act(out, in_, func, reads, writes, bias=0.0, scale=1.0, accum=None):
            P.op("act", lambda e: e.activation(out=out, in_=in_, func=func, bias=bias, scale=scale,
                                               accum_out=accum), reads, writes)

        def tcopy(eng, out, in_, reads, writes):
            P.op(eng, lambda e: e.tensor_copy(out=out, in_=in_), reads, writes)

        def tt(eng, out, in0, in1, op, reads, writes):
            P.op(eng, lambda e: e.tensor_tensor(out=out, in0=in0, in1=in1, op=op), reads, writes)

        def ts(eng, out, in0, s1, op0, reads, writes, s2=None, op1=None):
            if op1 is None:
                P.op(eng, lambda e: e.tensor_scalar(out=out, in0=in0, scalar1=s1, scalar2=None, op0=op0),
                     reads, writes)
            else:
                P.op(eng, lambda e: e.tensor_scalar(out=out, in0=in0, scalar1=s1, scalar2=s2, op0=op0, op1=op1),
                     reads, writes)

        def stt(eng, out, in0, scalar, in1, op0, op1, reads, writes):
            P.op(eng, lambda e: e.scalar_tensor_tensor(out=out, in0=in0, scalar=scalar, in1=in1, op0=op0, op1=op1),
                 reads, writes)

        def memset(eng, ap, val, writes):
            P.op(eng, lambda e: e.memset(ap, val), (), writes)

        def dma(out, in_, reads, writes, key, eng="sp"):
            P.op(eng, lambda e: e.dma_start(out=out, in_=in_), reads, writes, dma=key)

        def load_wA(name, idx, width=4096):
            b = cntA[0] % 2
            cntA[0] += 1
            dma(wA[b][:, 0:width], wtile(name, idx), [R_wbf], [R_wA[b]], "dwA%d" % b)
            return wA[b], R_wA[b]

        def load_wB(name, idx, width):
            b = cntB[0] % 2
            cntB[0] += 1
            dma(wB[b][:, 0:width], wtile(name, idx), [R_wbf], [R_wB[b]], "dwB%d" % b)
            return wB[b], R_wB[b]

        R_wbf = Res()

        dma(pc_t[:], pcat_d.ap(), [], [R_pc], "d_pc")
        dma(bc_t[:], bcat_d.ap().partition_broadcast(128), [], [R_bc], "d_bc")
        memset("pool", c_mhalf[:], -0.5, [R_const])
        memset("pool", Vaug[:], 1.0, [R_V])
        memset("pool", carry[:], 0.0, [R_carry])
        memset("pool", kcT[:], 0.0, [R_kc])
        memset("pool", hidv[:], 0.0, [R_hidv])
        memset("pool", vcmp[:], 0.0, [R_vc])
        for b in range(31):
            tt("dve", tabd[:, b * 16:(b + 1) * 16], bc_t[:, b * 16:(b + 1) * 16], bc_t[:, 496:512], ALU.subtract,
               [R_bc], [R_const])
        reset_local()
        ohbuf = sb([128, 8192], F32, "ohbuf"); R_oh = Res()
        work = sb([128, 16, 256], F32, "work"); R_work = Res()
        dma(ohbuf[:], ohT_d.ap(), [], [R_oh], "d_oh")
        for h in range(16):
            tcopy("dve" if h % 2 == 0 else "pool", work[:, h, :], ohbuf[:, 31 * 256:32 * 256], [R_oh], [R_work])
        for b in range(31):
            for h in range(16):
                stt("dve", work[:, h, :], ohbuf[:, b * 256:(b + 1) * 256], tabd[:, b * 16 + h:b * 16 + h + 1],
                    work[:, h, :], ALU.mult, ALU.add, [R_oh, R_const, R_work], [R_work])
        tcopy("dve", MT[:], work[:], [R_work], [R_mast])
        dma(ohbuf[:, 0:4864], ohC_d.ap(), [], [R_oh], "d_oh")
        wv = work[:].rearrange("p h y -> p (h y)")
        for h in range(16):
            tcopy("dve" if h % 2 == 0 else "pool", wv[:, h * 152:(h + 1) * 152], ohbuf[:, 31 * 152:32 * 152],
                  [R_oh], [R_work])
        for b in range(31):
            for h in range(16):
                stt("dve", wv[:, h * 152:(h + 1) * 152], ohbuf[:, b * 152:(b + 1) * 152],
                    tabd[:, b * 16 + h:b * 16 + h + 1], wv[:, h * 152:(h + 1) * 152], ALU.mult, ALU.add,
                    [R_oh, R_const, R_work], [R_work])
        tcopy("dve", MC[:].rearrange("p h x -> p (h x)"), wv[:, 0:16 * 152], [R_work], [R_mast])
        P.barrier(with_sp=True)

        reset_local()
        npp = wtotal // 128
        nconv = npp // CONV_W
        NST = 3
        stg_in = [sb([128, CONV_W], F32, "cin") for _ in range(NST)]
        stg_out = [sb([128, CONV_W], BF16, "cout") for _ in range(NST)]
        R_ci = [Res() for _ in range(NST)]
        R_co = [Res() for _ in range(NST)]
        for i in range(nconv):
            s_ = i % NST
            src = bass.AP(wcat, i * CONV_W, [[npp, 128], [1, CONV_W]])
            dst = bass.AP(wbf, i * CONV_W, [[npp, 128], [1, CONV_W]])
            dma(stg_in[s_][:], src, [], [R_ci[s_]], "d_ci%d" % s_)
            if i % 2 == 0:
                tcopy("dve", stg_out[s_][:], stg_in[s_][:], [R_ci[s_]], [R_co[s_]])
            else:
                act(stg_out[s_][:], stg_in[s_][:], AF.Copy, [R_ci[s_]], [R_co[s_]])
            dma(dst, stg_out[s_][:], [R_co[s_]], [R_wbf], "d_co%d" % s_, eng="pool")
        P.barrier(with_sp=True)
        for (t_, nm) in ((c_ident, "ident"), (c_ones, "ones"), (c_tril, "tril"), (c_wmask, "wmask"),
                         (c_emast, "emast"), (c_ov, "ov")):
            dma(t_[:], wtile(nm, 0), [R_wbf], [R_const], "d_c_" + nm)
        dma(c_w2ck[:].rearrange("p a b -> p (a b)"), wtile("w2ck", 0), [R_wbf], [R_const], "d_c_w2ck")
        dma(c_w2cv[:].rearrange("p a b -> p (a b)"), wtile("w2cv", 0), [R_wbf], [R_const], "d_c_w2cv")
        dma(c_peT[:].rearrange("p a b -> p (a b)"), wtile("peT", 0), [R_wbf], [R_const], "d_c_peT")
        for s_ in range(2):
            for pc in range(2):
                wt, rw = load_wA("w1c", (s_ * 2 + pc) * 2)
                mm(banks[0][:, 0:1], [(wt[0:64, l * 128:(l + 1) * 128], c_peT[0:64, s_, l:l + 1]) for l in range(32)],
                   [rw, R_const], [Rb[0]])
                tt("dve", cbias[:, s_ * 2 + pc:s_ * 2 + pc + 1], banks[0][:, 0:1], lnp("b1T", s_ * 2 + pc), ALU.add,
                   [Rb[0], R_pc], [R_const])
        P.barrier(with_sp=True)

        def layer_norm(idx, sq, nmean, msq, tbuf):
            for dc in range(8):
                tcopy("pool", hb[:, dc, :], h32[:, dc, :], [R_h[dc]], [R_hb[dc]])
                act(sq[0][:, dc, :], h32[:, dc, :], AF.Square, [R_h[dc]], [sq[1]])
            mm(banks[6][:], [(c_ones[:], hb[:, dc, :]) for dc in range(8)], R_hb + [R_const], [Rb[6]])
            mm(banks[7][:], [(c_ones[:], sq[0][:, dc, :]) for dc in range(8)], [sq[1], R_const], [Rb[7]])
            ts("dve", nmean[0][:], banks[6][:], -1.0 / D, ALU.mult, [Rb[6]], [nmean[1]])
            tt("dve", msq[0][:], nmean[0][:], nmean[0][:], ALU.mult, [nmean[1]], [msq[1]])
            stt("dve", msq[0][:], banks[7][:], 1.0 / D, msq[0][:], ALU.mult, ALU.subtract, [Rb[7], msq[1]], [msq[1]])
            ts("pool", msq[0][:], msq[0][:], EPS_LN / (ALPHA * ALPHA), ALU.add, [msq[1]], [msq[1]])
            tt("pool", msq[0][:], msq[0][:], c_mhalf[:], ALU.pow, [msq[1], R_const], [msq[1]])
            for dc in range(8):
                tb = tbuf[dc % 2]
                tt("dve", tb[0][:], h32[:, dc, :], nmean[0][:], ALU.add, [R_h[dc], nmean[1]], [tb[1]])
                tt("dve", tb[0][:], tb[0][:], msq[0][:], ALU.mult, [tb[1], msq[1]], [tb[1]])
                act(h32[:, dc, :], tb[0][:], AF.Identity, [tb[1], R_pc], [R_h[dc]],
                    bias=lnp("lnB", idx * 8 + dc), scale=lnp("lnG", idx * 8 + dc))
                tcopy("pool", hb[:, dc, :], h32[:, dc, :], [R_h[dc]], [R_hb[dc]])

        def ln_scratch():
            sq = (sb([128, 8, CH], BF16, "sq"), Res())
            nmean = (sb([128, CH], F32, "nmean"), Res())
            msq = (sb([128, CH], F32, "msq"), Res())
            tbuf = [(sb([128, CH], F32, "tb"), Res()) for _ in range(2)]
            return sq, nmean, msq, tbuf

        def ffn(l, j):
            reset_local()
            g = sb([128, NFC, CH], BF16, "g"); R_g = [Res() for _ in range(NFC)]
            sa = [(sb([128, CH], F32, "sa"), Res()) for _ in range(2)]
            lsc = ln_scratch()
            cres = 0.5 / ALPHA
            if do_ffn:
                for gi in range(11):
                    wt, rw = load_wA("w13_%d_%d" % (l, j), gi)
                    for fi in range(2):
                        fc = gi * 2 + fi
                        pa, ra = banks[fc % 2], Rb[fc % 2]
                        pb_, rb_ = banks[2 + fc % 2], Rb[2 + fc % 2]
                        mm(pa[:], [(wt[:, kc * 256 + fi * 128:kc * 256 + fi * 128 + 128], hb[:, kc, :]) for kc in range(8)],
                           [rw] + R_hb, [ra])
                        mm(pb_[:], [(wt[:, 2048 + kc * 256 + fi * 128:2048 + kc * 256 + fi * 128 + 128], hb[:, kc, :])
                                    for kc in range(8)], [rw] + R_hb, [rb_])
                        s_t, s_r = sa[fc % 2]
                        act(s_t[:], pa[:], AF.Silu, [ra], [s_r])
                        tt("dve", g[:, fc, :], s_t[:], pb_[:], ALU.mult, [s_r, rb_], [R_g[fc]])
                for dg in range(8):
                    wt, rw = load_wB("w2_%d_%d" % (l, j), dg, NFC * 128)
                    py, ry = banks[4 + dg % 2], Rb[4 + dg % 2]
                    mm(py[:], [(wt[:, fc * 128:(fc + 1) * 128], g[:, fc, :]) for fc in range(NFC)], [rw] + R_g, [ry])
                    stt("dve", h32[:, dg, :], py[:], cres, h32[:, dg, :], ALU.mult, ALU.add, [ry, R_h[dg]], [R_h[dg]])
            layer_norm(l * 3 + (0 if j == 0 else 2), *lsc)
            P.barrier()

        def sgu(l):
            a = l
            reset_local()
            u = sb([128, 16, CH], BF16, "u"); R_u = [Res() for _ in range(16)]
            vg_off = ptr[0]
            vg = sb([128, 4, 2048], F32, "vg"); R_vg = [Res() for _ in range(4)]
            vln = [(sb([128, 2048], BF16, "vln"), Res()) for _ in range(2)]
            addt = sb([128, 16, 128], F32, "addt"); R_addt = Res()
            stats = sb([128, 4, 24], F32, "stats"); R_stats = Res()
            mv = sb([128, 4, 2], F32, "mv"); R_mv = Res()
            rstd = sb([128, 4], F32, "rstd"); R_rstd = Res()
            tmpm = [(sb([128, 128], F32, "tmpm"), Res()) for _ in range(2)]
            dma(WT[:].rearrange("p g t -> p (g t)"), wtile("wsT_%d" % a, 0), [R_wbf], [R_WT], "d_WT")
            for gq in range(8):
                tt("pool", WT[:, gq, :], WT[:, gq, :], c_tril[:], ALU.mult, [R_WT, R_const], [R_WT])
            wtv = WT[:].rearrange("p g t -> p (g t)")
            mm(banks[6][:], [(c_ones[:], wtv[:, 0:512])], [R_WT, R_const], [Rb[6]])
            mm(banks[7][:], [(c_ones[:], wtv[:, 512:1024])], [R_WT, R_const], [Rb[7]])
            for cc in range(16):
                gq = cc // 2
                bk = banks[6 + gq // 4]
                stt("dve", addt[:, cc, :], bk[:, (gq % 4) * 128:(gq % 4) * 128 + 128], lnp("sgB", a * 16 + cc),
                    bc_t[:, 512 + (a * 8 + gq) * 128:512 + (a * 8 + gq) * 128 + 128], ALU.mult, ALU.add,
                    [Rb[6 + gq // 4], R_pc, R_bc], [R_addt])
            for gi in range(8):
                wt, rw = load_wA("win_%d" % a, gi, 2048)
                for fi in range(2):
                    cc = gi * 2 + fi
                    pu, ru = banks[cc % 2], Rb[cc % 2]
                    mm(pu[:], [(wt[:, kc * 256 + fi * 128:kc * 256 + fi * 128 + 128], hb[:, kc, :]) for kc in range(8)],
                       [rw] + R_hb, [ru])
                    act(u[:, cc, :], pu[:], AF.Gelu, [ru], [R_u[cc]])
            k = 0
            for gi in range(8):
                wt, rw = load_wA("win_%d" % a, 8 + gi, 2048)
                for t4 in range(4):
                    pv, rv = banks[2 + k % 2], Rb[2 + k % 2]
                    k += 1
                    mm(pv[:, 0:256], [(hb[:, kc, t4 * 128:(t4 + 1) * 128], wt[:, kc * 256:(kc + 1) * 256]) for kc in range(8)],
                       [rw] + R_hb, [rv])
                    act(vg[:, t4, gi * 256:(gi + 1) * 256], pv[:, 0:256], AF.Gelu, [rv], [R_vg[t4]])
            for t4 in range(4):
                for q in range(4):
                    P.op("dve", (lambda o, i: lambda e: e.bn_stats(out=o, in_=i))(stats[:, t4, q * 6:(q + 1) * 6],
                                                                               vg[:, t4, q * 512:(q + 1) * 512]),
                         [R_vg[t4]], [R_stats])
                P.op("dve", (lambda o, i: lambda e: e.bn_aggr(out=o, in_=i))(mv[:, t4, :], stats[:, t4, :]),
                     [R_stats], [R_mv])
                ts("pool", rstd[:, t4:t4 + 1], mv[:, t4, 1:2], EPS_LN, ALU.add, [R_mv], [R_rstd])
                tt("pool", rstd[:, t4:t4 + 1], rstd[:, t4:t4 + 1], c_mhalf[:, 0:1], ALU.pow, [R_rstd, R_const], [R_rstd])
                vl, rvl = vln[t4 % 2]
                ts("dve", vl[:], vg[:, t4, :], mv[:, t4, 0:1], ALU.subtract, [R_vg[t4], R_mv, R_rstd], [rvl],
                   s2=rstd[:, t4:t4 + 1], op1=ALU.mult)
                for cq in range(4):
                    pm, rm = banks[4 + cq % 2], Rb[4 + cq % 2]
                    mm_multi([(pm[:, ci * 128:(ci + 1) * 128], vl[:, (cq * 4 + ci) * 128:(cq * 4 + ci + 1) * 128],
                               WT[:, (cq * 4 + ci) // 2, :], True, True, False) for ci in range(4)],
                             [rvl, R_WT], [rm])
                    for ci in range(4):
                        cc = cq * 4 + ci
                        tm, rtm = tmpm[ci % 2]
                        stt("dve", tm[:], pm[:, ci * 128:(ci + 1) * 128], lnp("sgG", a * 16 + cc), addt[:, cc, :],
                            ALU.mult, ALU.add, [rm, R_pc, R_addt], [rtm])
                        tt("pool", u[:, cc, t4 * 128:(t4 + 1) * 128], tm[:], u[:, cc, t4 * 128:(t4 + 1) * 128], ALU.mult,
                           [rtm, R_u[cc]], [R_u[cc]])
            for dg in range(8):
                wt, rw = load_wB("wout_%d" % a, dg, 16 * 128)
                py, ry = banks[dg % 2], Rb[dg % 2]
                mm(py[:], [(wt[:, cc * 128:(cc + 1) * 128], u[:, cc, :]) for cc in range(16)], [rw] + R_u, [ry])
                stt("dve", h32[:, dg, :], py[:], 1.0 / ALPHA, h32[:, dg, :], ALU.mult, ALU.add, [ry, R_h[dg]], [R_h[dg]])
            P.barrier()
            ptr[0] = vg_off
            lsc = ln_scratch()
            layer_norm(l * 3 + 1, *lsc)
            P.barrier()

        def kv_prep(c):
            reset_local()
            T0 = c * CH
            kv0 = sb([128, 4, 528], BF16, "kv0"); R_kv0 = Res()
            hid = sb([128, 2, 2, 4, 32], BF16, "hid"); R_hid = Res()
            vct = sb([128, 4, 64], BF16, "vct"); R_vct = Res()
            for gi in range(4):
                wt, rw = load_wA("kk", gi, 2048)
                for fi in range(2):
                    fch = gi * 2 + fi
                    pk, rk = banks[fch % 2], Rb[fch % 2]
                    mm(pk[:], [(wt[:, kc * 256 + fi * 128:kc * 256 + fi * 128 + 128], hb[:, kc, :]) for kc in range(8)],
                       [rw] + R_hb, [rk])
                    if fch % 2 == 0:
                        tcopy("dve", kTs[:, fch, T0:T0 + CH], pk[:], [rk], [R_kTs])
                    else:
                        act(kTs[:, fch, T0:T0 + CH], pk[:], AF.Copy, [rk], [R_kTs])
            wt, rw = load_wA("kvv", 0)
            for t4 in range(4):
                pv, rv = banks[2 + t4 % 2], Rb[2 + t4 % 2]
                mm(pv[:], [(hb[:, kc, t4 * 128:(t4 + 1) * 128], wt[:, kc * 512:(kc + 1) * 512]) for kc in range(8)],
                   [rw] + R_hb, [rv])
                tcopy("dve", Vaug[:, 4 * c + t4, :, 0:64], pv[:].rearrange("p (a d) -> p a d", d=64), [rv], [R_V])
            if stage < 0.1:
                P.barrier()
                return
            tcopy("pool", kv0[:, :, 0:16], carry[:], [R_carry], [R_kv0])
            wt, rw = load_wA("kvc", 0)
            for fch in range(4):
                pc_, rc_ = banks[4 + fch % 2], Rb[4 + fch % 2]
                mm(pc_[:], [(wt[:, kc * 512 + fch * 128:kc * 512 + fch * 128 + 128], hb[:, kc, :]) for kc in range(8)],
                   [rw] + R_hb, [rc_])
                act(kv0[:, fch, 16:528], pc_[:], AF.Copy, [rc_], [R_kv0])
            tcopy("pool", carry[:], kv0[:, :, 512:528], [R_kv0], [R_carry])
            if stage < 0.3:
                P.barrier()
                return
            for s_ in range(2):
                for pc in range(2):
                    ph, rh = banks[6 + (s_ * 2 + pc) % 2], Rb[6 + (s_ * 2 + pc) % 2]
                    for half in range(2):
                        wt, rw = load_wA("w1c", (s_ * 2 + pc) * 2 + half)
                        for gq in (half, half + 2):
                            fch = s_ * 2 + gq // 2
                            mm(ph[:, gq * 32:(gq + 1) * 32],
                               [(wt[:, l * 128:(l + 1) * 128], kv0[:, fch, l:l + 497:16]) for l in range(32)],
                               [rw, R_kv0], [rh])
                    if s_ == 0:
                        act(hid[:, s_, pc, :, :].rearrange("p g n -> p (g n)"), ph[:, 0:128], AF.Gelu, [rh, R_const], [R_hid],
                            bias=cbias[:, s_ * 2 + pc:s_ * 2 + pc + 1])
                    else:
                        act(hidv[:, pc, :, 32 * c:32 * c + 32], ph[:, 0:128].rearrange("p (g n) -> p g n", n=32), AF.Gelu,
                            [rh, R_const], [R_hidv], bias=cbias[:, s_ * 2 + pc:s_ * 2 + pc + 1])
            if stage < 0.5:
                P.barrier()
                return
            pk, rk = banks[0], Rb[0]
            for gq in range(4):
                mm(pk[:, gq * 32:(gq + 1) * 32], [(c_w2ck[:, pc, :], hid[:, 0, pc, gq, :]) for pc in range(2)],
                   [R_hid, R_const], [rk])
            tcopy("dve", kcT[:, :, 32 * c:32 * c + 32], pk[:, 0:128].rearrange("p (g n) -> p g n", n=32), [rk], [R_kc])
            if stage < 0.7:
                P.barrier()
                return
            NV = 32 * (c + 1)
            pv, rv = banks[1], Rb[1]
            for gq in range(4):
                mm(pv[:, gq * 64:(gq + 1) * 64], [(hidv[:, pc, gq, :], c_w2cv[:, pc, :]) for pc in range(2)],
                   [R_hidv, R_const], [rv])
            if stage != 0.8:
                if stage == 0.9:
                    tcopy("dve", kv0[:, 0, 0:256], pv[:, 0:256], [rv], [R_kv0])
                else:
                    act(vcmp[:].rearrange("p g d -> p (g d)"), pv[:, 0:256], AF.Copy, [rv], [R_vc])
            P.barrier()

        def nsa(l, c):
            bl = l - 2
            reset_local()
            T0 = c * CH
            NV = 32 * (c + 1)
            q_off = ptr[0]
            qT = sb([128, 8, 2, CH], BF16, "qT"); R_q = [Res() for _ in range(8)]
            gates = sb([128, 4, 48], F32, "gates"); R_gates = Res()
            snT = sb([128, 4, CH], BF16, "snT"); R_sn = [Res() for _ in range(4)]
            Pe = [(sb([128, 128], F32, "Pe"), Res()) for _ in range(2)]
            Pn = [(sb([128, 128], BF16, "Pn"), Res()) for _ in range(2)]
            PT = [(sb([128, 128], BF16, "PT"), Res()) for _ in range(2)]
            ET = [(sb([128, CH], BF16, "ET"), Res()) for _ in range(3)]
            small = sb([128, 64], F32, "small"); R_small = Res()
            imp_t = sb([128, 32], F32, "imp"); R_imp = Res()
            top8 = sb([128, 8], F32, "top8")
            snq = sb([128, 32], BF16, "snq"); R_snq = Res()
            o_acc = sb([128, 4, D], F32, "oacc"); R_oacc = [Res() for _ in range(4)]
            o_bf = sb([128, 4, D], BF16, "obf"); R_obf = [Res() for _ in range(4)]
            oTt = sb([128, 8, CH], BF16, "oT"); R_oT = [Res() for _ in range(8)]
            rdc = sb([128, 2, 8], F32, "rdc"); R_rdc = [Res(), Res()]
            for b_ in range(2):
                memset("pool", Pn[b_][0][:], 0.0, [Pn[b_][1]])
            for hc_ in range(8):
                memset("pool", qT[:, hc_, :, :], 0.0, [R_q[hc_]])
            if stage < 2:
                P.barrier()
                ptr[0] = q_off
                layer_norm(l * 3 + 1, *ln_scratch())
                P.barrier()
                return
            for gi in range(4):
                wt, rw = load_wA("wq_%d" % bl, gi, 2048)
                for fi in range(2):
                    hc = gi * 2 + fi
                    pq, rq = banks[hc % 2], Rb[hc % 2]
                    mm(pq[:], [(wt[:, kc * 256 + fi * 128:kc * 256 + fi * 128 + 128], hb[:, kc, :]) for kc in range(8)],
                       [rw] + R_hb, [rq])
                    act(qT[0:64, hc, 0, :], pq[0:64, :], AF.Copy, [rq], [R_q[hc]], scale=0.125)
                    act(qT[64:128, hc, 1, :], pq[64:128, :], AF.Copy, [rq], [R_q[hc]], scale=0.125)
            dma(wgt[:].rearrange("p a b -> p (a b)"), wtile("wg_%d" % bl, 0), [R_wbf], [R_wgt], "d_wgt")
            pg, rg = banks[2], Rb[2]
            for t4 in range(4):
                mm(pg[:, t4 * 48:(t4 + 1) * 48], [(hb[:, kc, t4 * 128:(t4 + 1) * 128], wgt[:, kc, :]) for kc in range(8)],
                   [R_wgt] + R_hb, [rg])
            act(gates[:].rearrange("p a b -> p (a b)"), pg[:, 0:192], AF.Sigmoid, [rg], [R_gates])
            if stage < 3:
                P.barrier()
                ptr[0] = q_off
                layer_norm(l * 3 + 1, *ln_scratch())
                P.barrier()
                return
            k = 0
            for qt in range(4):
                QT = 4 * c + qt
                xoff = 120 - 8 * QT
                ioff = 30 - 2 * QT
                for gq in range(4):
                    pimp, rimp = banks[3], Rb[3]
                    for r in range(4):
                        h = 4 * gq + r
                        hc = h // 2
                        r0 = (h % 2) * 64
                        psc, rsc = banks[k % 2], Rb[k % 2]
                        pe_t, pe_r = Pe[k % 2]
                        pn_t, pn_r = Pn[k % 2]
                        pt_t, pt_r = PT[k % 2]
                        k += 1
                        mm_multi([(psc[:, 0:NV], qT[:, hc, h % 2, qt * 128:(qt + 1) * 128], kcT[:, gq, 0:NV],
                                   True, False, False),
                                  (psc[:, 0:NV], c_ident[:], MC[:, h, xoff:xoff + NV], False, True, False)],
                                 [R_q[hc], R_kc, R_const, R_mast], [rsc])
                        sm = small[:, (k % 2) * 8:(k % 2) * 8 + 8]
                        P.op("dve", (lambda o, i: lambda e: e.reduce_max(out=o, in_=i, axis=AX.X))(sm[:, 0:1], psc[:, 1:NV]),
                             [rsc], [R_small])
                        ts("dve", sm[:, 1:2], sm[:, 0:1], -1.0, ALU.mult, [R_small], [R_small])
                        act(pe_t[:, 1:NV], psc[:, 1:NV], AF.Exp, [rsc, R_small], [pe_r, R_small], bias=sm[:, 1:2],
                            accum=sm[:, 2:3])
                        P.op("dve", (lambda o, i: lambda e: e.reciprocal(out=o, in_=i))(sm[:, 3:4], sm[:, 2:3]),
                             [R_small], [R_small])
                        if QT == 0:
                            tt("dve", sm[:, 3:4], sm[:, 3:4], lnp("rv0", 0), ALU.mult, [R_small, R_pc], [R_small])
                        ts("dve", pn_t[:, 1:NV], pe_t[:, 1:NV], sm[:, 3:4], ALU.mult, [pe_r, R_small], [pn_r])
                        ptp = banks[2][:, 256:320].bitcast(BF16)
                        P.op("pe", (lambda o, i: lambda e: e.transpose(o, i, c_ident[:]))(ptp[0:NV, :], pn_t[:, 0:NV]),
                             [pn_r, R_const], [Rb[2]])
                        tcopy("dve", pt_t[0:NV, :], ptp[0:NV, :], [Rb[2]], [pt_r])
                        poc = banks[2][:, 320 + (r % 2) * 64:320 + (r % 2) * 64 + 64]
                        mm(poc, [(pt_t[0:NV, :], vcmp[0:NV, gq, :])], [pt_r, R_vc], [Rb[2]])
                        mm(pimp[:, 0:32], [(pt_t[0:NV, :], c_ov[0:NV, :])], [pt_r, R_const], [rimp],
                           start=(r == 0), stop=(r == 3))
                        ts("dve", o_acc[:, qt, h * 64:(h + 1) * 64], poc, gates[:, qt, h * 3:h * 3 + 1], ALU.mult,
                           [Rb[2], R_gates], [R_oacc[qt]])
                    o_vm, _ = pcols["VM"]
                    o_fb, _ = pcols["FB"]
                    tt("dve", imp_t[:], pimp[:, 0:32], pc_t[:, o_vm + ioff:o_vm + ioff + 32], ALU.mult, [rimp, R_pc], [R_imp])
                    tt("dve", imp_t[:], imp_t[:], pc_t[:, o_fb + ioff:o_fb + ioff + 32], ALU.add, [R_imp, R_pc], [R_imp])
                    memset("dve", imp_t[:, 0:1], BIG, [R_imp])
                    P.op("dve", (lambda o, i: lambda e: e.max(out=o, in_=i))(top8[:], imp_t[:]), [R_imp], [R_small])
                    ts("dve", imp_t[:], imp_t[:], top8[:, 7:8], ALU.is_lt, [R_imp, R_small], [R_imp])
                    ts("dve", snq[:], imp_t[:], NEGM, ALU.mult, [R_imp], [R_snq])
                    pst = banks[2][:, 448:512].bitcast(BF16)
                    P.op("pe", (lambda o, i: lambda e: e.transpose(o, i, c_ident[:]))(pst[0:32, :], snq[:]),
                         [R_snq, R_const], [Rb[2]])
                    tcopy("dve", snT[0:32, gq, qt * 128:(qt + 1) * 128], pst[0:32, :], [Rb[2]], [R_sn[gq]])
            if stage < 4 and stage not in (3.2, 3.4, 3.6):
                P.barrier()
                ptr[0] = q_off
                layer_norm(l * 3 + 1, *ln_scratch())
                P.barrier()
                return
            o_c31 = 496
            kst = 0
            ke = 0
            for h in range(16):
                gq = h // 4
                hc = h // 2
                r0 = (h % 2) * 64
                for br in range(2):
                    if (stage == 3.2 and br == 1) or (stage == 3.4 and br == 0):
                        continue
                    pO, rO = banks[6 + br], Rb[6 + br]
                    P.op("dve", (lambda o: lambda e: e.memset(o, 0.0))(pO[:, 0:260]), [], [rO])
                    kt_lo = 0 if br == 0 else max(0, 4 * c - 4)
                    for kt in range(kt_lo, 4 * c + 4):
                        bs = [b for b in range(4) if (4 * c + b - kt) >= 0 and (br == 0 or (4 * c + b - kt) <= 4)]
                        if not bs:
                            continue
                        b_lo, b_hi = bs[0], bs[-1]
                        c0, c1 = b_lo * 128, (b_hi + 1) * 128
                        pst_, rst_ = banks[4 + kst % 2], Rb[4 + kst % 2]
                        kst += 1
                        items = [(pst_[:, c0:c1], kTs[:, br * 4 + gq, kt * 128:(kt + 1) * 128],
                                  qT[:, hc, h % 2, c0:c1], True, False, False)]
                        reads = [R_kTs, R_q[hc], R_const, R_mast]
                        if br == 0:
                            items.append((pst_[:, c0:c1], c_emast[0:32, kt * 128:(kt + 1) * 128], snT[0:32, gq, c0:c1],
                                          False, False, False))
                            reads.append(R_sn[gq])
                        for b in bs:
                            d = 4 * c + b - kt
                            if d == 0:
                                items.append((pst_[:, b * 128:(b + 1) * 128], c_ident[:], MT[:, h, 0:128], False, False, False))
                            elif d == 1:
                                items.append((pst_[:, b * 128:(b + 1) * 128], c_ident[:], MT[:, h, 128:256], False, False, False))
                            elif d == 4 and br == 1 and stage != 3.6:
                                items.append((pst_[:, b * 128:(b + 1) * 128], c_ident[:], c_wmask[:], False, False, False))
                        it = items[-1]
                        items[-1] = (it[0], it[1], it[2], it[3], True, it[5])
                        mm_multi(items, reads, [rst_])
                        et_t, et_r = ET[ke % 3]
                        ke += 1
                        act(et_t[:, c0:c1], pst_[:, c0:c1], AF.Exp, [rst_, R_bc], [et_r], bias=bc_t[:, o_c31 + h:o_c31 + h + 1])
                        mm_multi([(pO[:, b * 65:(b + 1) * 65], et_t[:, b * 128:(b + 1) * 128], Vaug[:, kt, br * 4 + gq, :],
                                   False, False, True) for b in bs], [et_r, R_V], [rO])
                    rd_t = rdc[:, br, :]
                    dview = pO[:, 0:260].rearrange("p (b x) -> p b x", x=65)
                    P.op("dve", (lambda o, i: lambda e: e.reciprocal(out=o, in_=i))(rd_t[:, 0:4], dview[:, :, 64]),
                         [rO], [R_rdc[br]])
                    tt("dve", rd_t[:, 4:8], rd_t[:, 0:4], gates[:, :, h * 3 + 1 + br], ALU.mult, [R_rdc[br], R_gates], [R_rdc[br]])
                    for b in range(4):
                        stt("dve", o_acc[:, b, h * 64:(h + 1) * 64], pO[:, b * 65:b * 65 + 64], rd_t[:, 4 + b:5 + b],
                            o_acc[:, b, h * 64:(h + 1) * 64], ALU.mult, ALU.add, [rO, R_rdc[br], R_oacc[b]], [R_oacc[b]])
            if stage < 5 and stage not in (3.2, 3.4, 3.6):
                P.barrier()
                ptr[0] = q_off
                layer_norm(l * 3 + 1, *ln_scratch())
                P.barrier()
                return
            for b in range(4):
                tcopy("pool", o_bf[:, b, :], o_acc[:, b, :], [R_oacc[b]], [R_obf[b]])
            for hc in range(8):
                ptr_, rtr_ = banks[hc % 2], Rb[hc % 2]
                ptv = ptr_[:, 0:256].bitcast(BF16)
                for b in range(4):
                    P.op("pe", (lambda o, i: lambda e: e.transpose(o, i, c_ident[:]))(
                        ptv[:, b * 128:(b + 1) * 128], o_bf[:, b, hc * 128:(hc + 1) * 128]), [R_obf[b], R_const], [rtr_])
                if hc % 2 == 0:
                    tcopy("dve", oTt[:, hc, :], ptv[:, 0:512], [rtr_], [R_oT[hc]])
                else:
                    act(oTt[:, hc, :], ptv[:, 0:512], AF.Copy, [rtr_], [R_oT[hc]])
            for gi in range(4):
                wt, rw = load_wA("wo_%d" % bl, gi, 2048)
                for fi in range(2):
                    dc = gi * 2 + fi
                    py, ry = banks[2 + dc % 2], Rb[2 + dc % 2]
                    mm(py[:], [(wt[:, kc * 256 + fi * 128:kc * 256 + fi * 128 + 128], oTt[:, kc, :]) for kc in range(8)],
                       [rw] + R_oT, [ry])
                    stt("dve", h32[:, dc, :], py[:], 1.0 / ALPHA, h32[:, dc, :], ALU.mult, ALU.add, [ry, R_h[dc]], [R_h[dc]])
            P.barrier()
            ptr[0] = q_off
            layer_norm(l * 3 + 1, *ln_scratch())
            P.barrier()

        R_out = Res()
        for s_ in range(nseq):
            for c in range(nchunk):
                T0 = c * CH
                if c == 0:
                    memset("pool", carry[:], 0.0, [R_carry])
                src = xT.ap()[s_].rearrange("(dc p) t -> p dc t", p=128)[:, :, T0:T0 + CH]
                dma(h32[:], src, [], R_h, "d_x")
                for dc in range(8):
                    tcopy("pool" if dc % 2 else "dve", hb[:, dc, :], h32[:, dc, :], [R_h[dc]], [R_hb[dc]])
                for l in layers:
                    if l == 2 and do_mixer:
                        kv_prep(c)
                    ffn(l, 0)
                    if do_mixer:
                        if l < 2:
                            sgu(l)
                        else:
                            nsa(l, c)
                    else:
                        reset_local()
                        lsc = ln_scratch()
                        layer_norm(l * 3 + 1, *lsc)
                        P.barrier()
                    ffn(l, 1)
                dst = oT.ap()[s_].rearrange("(dc p) t -> p dc t", p=128)[:, :, T0:T0 + CH]
                dma(dst, h32[:], R_h, [R_out], "d_out")
        P.final_wait("sp")
        P.replay(st)
    return nc


_CFG = {}


def kernel(**inputs):
    inp = {k: np.asarray(v) for k, v in inputs.items()}
    x = inp["x"].astype(np.float32, copy=False)
    wc = _wlayout(inp)
    wflat = wc.finish()
    pcat, pcols, bcat, ohT, ohC = _small_params(inp)
    nc = build_nc(wc.off, wflat.size, pcols, pcat.shape[1], _CFG)
    in_maps = []
    ncores = _CFG.get("ncores", NCORES)
    for core in range(ncores):
        xs = np.ascontiguousarray(x[core * BPC:(core + 1) * BPC].transpose(0, 2, 1))
        in_maps.append({"xT": xs, "wcat": wflat, "pcat": pcat, "bcat": bcat, "ohT": ohT, "ohC": ohC})
    res = run_bass_kernel_spmd(nc, in_maps, core_ids=list(range(ncores)))
    out = np.zeros((NCORES * BPC, S, D), np.float32)
    for core in range(ncores):
        o = np.asarray(res.results[core]["oT"])
        out[core * BPC:(core + 1) * BPC] = o.transpose(0, 2, 1)
    return out
```

```python
import math
from contextlib import ExitStack
import numpy as np
import concourse.bass as bass
import concourse.mybir as mybir
from concourse.bass_utils import run_bass_kernel_spmd

F32 = mybir.dt.float32
BF16 = mybir.dt.bfloat16
AF = mybir.ActivationFunctionType
ALU = mybir.AluOpType
AX = mybir.AxisListType

D = 1024
S = 2048
DEPTH = 4
FF = 2816
NFC = 22
CH = 512
NCH = S // CH
NCORES = 8
BPC = 4
ALPHA = (2.0 * DEPTH) ** 0.25
EPS_LN = 1e-5
NEGM = -32768.0
BIG = 1e30
SB_BASE = 16512
SB_LIMIT = 229344
CONV_W = 2048

ENGS = ("pe", "act", "dve", "pool", "sp")
CENGS = ("pe", "act", "dve", "pool")


class Res:
    __slots__ = ("w", "rs")

    def __init__(self):
        self.w = None
        self.rs = {}


class Prog:
    def __init__(self, nc):
        self.nc = nc
        self.ops = {e: [] for e in ENGS}
        self.cnt = {}
        self.waited = {e: {} for e in ENGS}

    def op(self, eng, fn, reads=(), writes=(), dma=None):
        deps = {}
        for r in reads:
            if r.w is not None:
                k, v = r.w
                if deps.get(k, 0) < v:
                    deps[k] = v
        for w in writes:
            if w.w is not None:
                k, v = w.w
                if deps.get(k, 0) < v:
                    deps[k] = v
            for k, v in w.rs.items():
                if deps.get(k, 0) < v:
                    deps[k] = v
        waits = []
        wd = self.waited[eng]
        for k, v in deps.items():
            if k == eng and eng == "pe":
                continue
            if wd.get(k, 0) >= v:
                continue
            wd[k] = v
            waits.append((k, v))
        if dma is not None:
            key, inc = dma, 16
        else:
            key, inc = eng, 1
        v = self.cnt.get(key, 0) + inc
        self.cnt[key] = v
        self.ops[eng].append((waits, fn, key, inc))
        for r in reads:
            if r.rs.get(key, 0) < v:
                r.rs[key] = v
        for w in writes:
            w.w = (key, v)
            w.rs = {}

    def barrier(self, with_sp=False):
        snap = dict(self.cnt)
        for e in (ENGS if with_sp else CENGS):
            waits = []
            wd = self.waited[e]
            for k, v in snap.items():
                if k == e and e == "pe":
                    continue
                if wd.get(k, 0) >= v:
                    continue
                wd[k] = v
                waits.append((k, v))
            if waits:
                self.ops[e].append((waits, None, None, 0))

    def final_wait(self, eng):
        snap = dict(self.cnt)
        self.ops[eng].append((list(snap.items()), None, None, 0))

    def replay(self, stack):
        nc = self.nc
        sems = {}
        for k in self.cnt.keys():
            sems[k] = stack.enter_context(nc.semaphore("s_" + str(k)))
        block = stack.enter_context(nc.Block())

        def run(engine, lst):
            for waits, fn, sk, inc in lst:
                for k, v in waits:
                    engine.wait_ge(sems[k], v)
                if fn is not None:
                    fn(engine).then_inc(sems[sk], inc)

        @block.tensor
        def _(e):
            run(e, self.ops["pe"])

        @block.scalar
        def _(e):
            run(e, self.ops["act"])

        @block.vector
        def _(e):
            run(e, self.ops["dve"])

        @block.gpsimd
        def _(e):
            run(e, self.ops["pool"])

        @block.sync
        def _(e):
            run(e, self.ops["sp"])


def _rel_bucket(dist):
    n = np.maximum(dist, 0)
    nf = np.maximum(n, 1).astype(np.float32)
    large = 16 + (np.log(nf / np.float32(16)) / np.float32(math.log(128 / 16)) * np.float32(16)).astype(np.int32)
    return np.where(n < 16, n, np.minimum(large, 31)).astype(np.int64)


class WCat:
    def __init__(self):
        self.parts = []
        self.off = {}
        self.n = 0

    def add(self, name, arr):
        arr = np.ascontiguousarray(arr, dtype=np.float32)
        assert arr.shape[1] == 128, (name, arr.shape)
        x = int(np.prod(arr.shape[2:]))
        self.off[name] = (self.n, x, arr.shape[0])
        self.parts.append(arr.reshape(-1))
        self.n += arr.size

    def finish(self):
        blk = 128 * CONV_W
        pad = (-self.n) % blk
        if pad:
            self.parts.append(np.zeros(pad, np.float32))
            self.n += pad
        return np.concatenate(self.parts)


def _wlayout(inp):
    wc = WCat()
    for l in range(DEPTH):
        for j in range(2):
            w1 = inp["ffn_w1"][l, j].reshape(8, 128, 11, 256).transpose(2, 1, 0, 3)
            w3 = inp["ffn_w3"][l, j].reshape(8, 128, 11, 256).transpose(2, 1, 0, 3)
            w13 = np.concatenate([w1.reshape(11, 128, 2048), w3.reshape(11, 128, 2048)], axis=2)
            wc.add("w13_%d_%d" % (l, j), w13)
            wc.add("w2_%d_%d" % (l, j), inp["ffn_w2"][l, j].reshape(22, 128, 8, 128).transpose(2, 1, 0, 3))
    for a in range(2):
        wc.add("win_%d" % a, inp["sgu_w_in"][a].reshape(8, 128, 16, 256).transpose(2, 1, 0, 3))
        wc.add("wout_%d" % a, inp["sgu_w_out"][a].reshape(16, 128, 8, 128).transpose(2, 1, 0, 3))
        wc.add("wsT_%d" % a, inp["sgu_w_s"][a].transpose(2, 0, 1)[None])
    kv = inp["kv_w"].reshape(D, 6, 4, 64)
    kk = np.stack([kv[:, 2], kv[:, 4]], axis=1)
    kk = np.repeat(kk[:, :, :, None, :], 2, axis=3).reshape(D, 1024)
    wc.add("kk", kk.reshape(8, 128, 4, 256).transpose(2, 1, 0, 3))
    kvv = np.stack([kv[:, 3], kv[:, 5]], axis=1).reshape(D, 512)
    wc.add("kvv", kvv.reshape(8, 128, 512).transpose(1, 0, 2)[None])
    kvc = np.stack([kv[:, 0], kv[:, 1]], axis=1).reshape(D, 512)
    wc.add("kvc", kvc.reshape(8, 128, 512).transpose(1, 0, 2)[None])
    w1c = inp["cmp_w1"].reshape(2, 32, 64, 2, 128).transpose(0, 3, 2, 1, 4)
    zz = np.zeros_like(w1c)
    w1c = np.stack([np.concatenate([w1c, zz], axis=2), np.concatenate([zz, w1c], axis=2)], axis=2)
    wc.add("w1c", w1c.reshape(8, 128, 32, 128))
    w2k = inp["cmp_w2"][0].reshape(2, 128, 64).transpose(1, 0, 2)
    wc.add("w2ck", np.concatenate([w2k, w2k], axis=2)[None])
    wc.add("w2cv", inp["cmp_w2"][1].reshape(2, 128, 64).transpose(1, 0, 2)[None])
    peT = inp["cmp_pe"].transpose(2, 0, 1)
    wc.add("peT", np.concatenate([peT, peT], axis=0)[None])
    for bl in range(2):
        wq = inp["nsa_w_qg"][bl][:, :1024]
        wc.add("wq_%d" % bl, wq.reshape(8, 128, 4, 256).transpose(2, 1, 0, 3))
        wg = inp["nsa_w_qg"][bl][:, 1024:]
        wc.add("wg_%d" % bl, wg.reshape(8, 128, 48).transpose(1, 0, 2)[None])
        wc.add("wo_%d" % bl, inp["nsa_w_o"][bl].reshape(8, 128, 4, 256).transpose(2, 1, 0, 3))
    ii = np.arange(128)
    wc.add("ident", np.eye(128, dtype=np.float32)[None])
    wc.add("ones", np.ones((1, 128, 128), np.float32))
    wc.add("tril", (ii[None, :] >= ii[:, None]).astype(np.float32)[None])
    wc.add("wmask", np.where(ii[None, :] >= ii[:, None], NEGM, 0.0).astype(np.float32)[None])
    em = np.zeros((128, 2048), np.float32)
    for j in range(32):
        em[j, 64 * j:64 * j + 64] = 1.0
    wc.add("emast", em[None])
    ov = np.zeros((128, 32), np.float32)
    for m in range(1, 128):
        for j in range(32):
            if 4 * j <= m <= 4 * j + 4:
                ov[m, j] = 1.0
    wc.add("ov", ov[None])
    return wc


def _small_params(inp):
    cols = {}
    parts = []
    n = 0

    def add(name, a):
        nonlocal n
        a = np.ascontiguousarray(a, np.float32).reshape(128, -1)
        cols[name] = (n, a.shape[1])
        parts.append(a)
        n += a.shape[1]

    add("lnG", inp["ln_g"].reshape(12, 8, 128).transpose(2, 0, 1))
    add("lnB", inp["ln_b"].reshape(12, 8, 128).transpose(2, 0, 1))
    add("sgG", inp["sgu_ln_g"].reshape(2, 16, 128).transpose(2, 0, 1))
    add("sgB", inp["sgu_ln_b"].reshape(2, 16, 128).transpose(2, 0, 1))
    add("b1T", inp["cmp_b1"].reshape(2, 2, 128).transpose(2, 0, 1))
    ii = np.arange(128)
    add("rv0", (ii >= 31).astype(np.float32)[:, None])
    ci = (ii >= 64).astype(np.int64)[:, None]
    x = np.arange(-30, 32)[None, :]
    forced = (x == ci - 1) | (x == ci)
    invalid = x > ci
    vm = (~forced & ~invalid).astype(np.float32)
    fb = np.where(forced, BIG, np.where(invalid, -BIG, 0.0)).astype(np.float32)
    add("VM", vm)
    add("FB", fb)
    pcat = np.concatenate(parts, axis=1)
    bcat = np.concatenate([inp["rel_table"].reshape(-1), inp["sgu_b_s"].reshape(-1)]).astype(np.float32)[None]
    j = ii[:, None]
    y = np.arange(256)[None, :]
    dT = y - j
    bT = _rel_bucket(dT)
    oht = np.stack([((bT == b) & (dT >= 0)).astype(np.float32) for b in range(31)], axis=1)
    negt = np.where(dT < 0, NEGM, 0.0).astype(np.float32)
    xx = np.arange(-120, 32)[None, :]
    dC = j - 16 * xx - 15
    bC = _rel_bucket(dC)
    ohc = np.stack([((bC == b) & (dC >= 0)).astype(np.float32) for b in range(31)], axis=1)
    negc = np.where(dC < 0, NEGM, 0.0).astype(np.float32)
    ohT = np.concatenate([oht.reshape(128, -1), negt], axis=1)
    ohC = np.concatenate([ohc.reshape(128, -1), negc], axis=1)
    return pcat, cols, bcat, np.ascontiguousarray(ohT), np.ascontiguousarray(ohC)


DTSIZE = {F32: 4, BF16: 2}


def build_nc(woff, wtotal, pcols, npcat, cfg):
    nseq = cfg.get("nseq", BPC)
    nchunk = cfg.get("nchunk", NCH)
    layers = cfg.get("layers", list(range(DEPTH)))
    do_mixer = cfg.get("mixer", True)
    do_ffn = cfg.get("ffn", True)
    stage = cfg.get("nsa_stage", 9)

    nc = bass.Bass("TRN2", target_bir_lowering=False)
    xT = nc.dram_tensor("xT", [BPC, D, S], F32, kind="ExternalInput")
    wcat = nc.dram_tensor("wcat", [wtotal], F32, kind="ExternalInput")
    pcat_d = nc.dram_tensor("pcat", [128, npcat], F32, kind="ExternalInput")
    bcat_d = nc.dram_tensor("bcat", [1, 2560], F32, kind="ExternalInput")
    ohT_d = nc.dram_tensor("ohT", [128, 8192], F32, kind="ExternalInput")
    ohC_d = nc.dram_tensor("ohC", [128, 4864], F32, kind="ExternalInput")
    oT = nc.dram_tensor("oT", [BPC, D, S], F32, kind="ExternalOutput")
    wbf = nc.dram_tensor("wbf", [wtotal], BF16)

    P = Prog(nc)
    uid = [0]
    ptr = [SB_BASE]

    def sb(shape, dt, name="t"):
        uid[0] += 1
        size = int(np.prod(shape[1:])) * DTSIZE[dt]
        size = (size + 31) // 32 * 32
        off = ptr[0]
        assert off + size <= SB_LIMIT, ("SBUF overflow", name, off, size)
        ptr[0] = off + size
        return nc.alloc_sbuf_tensor_at("%s_%d" % (name, uid[0]), list(shape), dt, offset=off)

    def wtile(name, idx):
        off, x, nt = woff[name]
        assert idx < nt
        return bass.AP(wbf, off + idx * 128 * x, [[x, 128], [1, x]])

    with ExitStack() as st:
        banks = [st.enter_context(nc.psum_tensor("bank%d" % i, [128, 512], F32)) for i in range(8)]
        Rb = [Res() for _ in range(8)]

        pc_t = sb([128, npcat], F32, "pcat"); R_pc = Res()
        bc_t = sb([128, 2560], F32, "bcat"); R_bc = Res()
        c_ident = sb([128, 128], BF16, "ident")
        c_ones = sb([128, 128], BF16, "ones")
        c_tril = sb([128, 128], BF16, "tril")
        c_wmask = sb([128, 128], BF16, "wmask")
        c_emast = sb([128, 2048], BF16, "emast")
        c_ov = sb([128, 32], BF16, "ov")
        c_w2ck = sb([128, 2, 128], BF16, "w2ck")
        c_w2cv = sb([128, 2, 64], BF16, "w2cv")
        c_peT = sb([128, 2, 32], BF16, "peT")
        c_mhalf = sb([128, 512], F32, "mhalf")
        tabd = sb([128, 512], F32, "tabd")
        cbias = sb([128, 4], F32, "cbias")
        R_const = Res()
        MT = sb([128, 16, 256], BF16, "MT")
        MC = sb([128, 16, 152], BF16, "MC")
        R_mast = Res()
        kTs = sb([128, 8, S], BF16, "kTs"); R_kTs = Res()
        Vaug = sb([128, 16, 8, 65], BF16, "Vaug"); R_V = Res()
        kcT = sb([128, 4, 128], BF16, "kcT"); R_kc = Res()
        hidv = sb([128, 2, 4, 128], BF16, "hidv"); R_hidv = Res()
        vcmp = sb([128, 4, 64], BF16, "vcmp"); R_vc = Res()
        carry = sb([128, 4, 16], BF16, "carry"); R_carry = Res()
        h32 = sb([128, 8, CH], F32, "h32"); R_h = [Res() for _ in range(8)]
        hb = sb([128, 8, CH], BF16, "hb"); R_hb = [Res() for _ in range(8)]
        wA = [sb([128, 4096], BF16, "wA") for _ in range(2)]; R_wA = [Res(), Res()]
        wB = [sb([128, 22 * 128], BF16, "wB") for _ in range(2)]; R_wB = [Res(), Res()]
        WT = sb([128, 8, 128], BF16, "WT"); R_WT = Res()
        wgt = sb([128, 8, 48], BF16, "wgt"); R_wgt = Res()
        local_base = ptr[0]
        cntA = [0]
        cntB = [0]

        def lnp(name, idx):
            o, n = pcols[name]
            return pc_t[:, o + idx:o + idx + 1]

        def pslice(name):
            o, n = pcols[name]
            return pc_t[:, o:o + n]

        def reset_local():
            ptr[0] = local_base

        def mm(out_ap, pairs, reads, writes, start=True, stop=True, skip=False):
            pairs = list(pairs)

            def fn(e):
                n = len(pairs)
                ins = None
                for i, (l, r) in enumerate(pairs):
                    ins = e.matmul(out_ap, l, r, start=(start and i == 0), stop=(stop and i == n - 1),
                                   skip_group_check=skip)
                return ins
            P.op("pe", fn, reads, writes)

        def mm_multi(items, reads, writes):
            items = list(items)

            def fn(e):
                ins = None
                for (o, l, r, s0, s1, sk) in items:
                    ins = e.matmul(o, l, r, start=s0, stop=s1, skip_group_check=sk)
                return ins
            P.op("pe", fn, reads, writes)

        def act(out, in_, func, reads, writes, bias=0.0, scale=1.0, accum=None):
            P.op("act", lambda e: e.activation(out=out, in_=in_, func=func, bias=bias, scale=scale,
                                               accum_out=accum), reads, writes)

        def tcopy(eng, out, in_, reads, writes):
            P.op(eng, lambda e: e.tensor_copy(out=out, in_=in_), reads, writes)

        def tt(eng, out, in0, in1, op, reads, writes):
            P.op(eng, lambda e: e.tensor_tensor(out=out, in0=in0, in1=in1, op=op), reads, writes)

        def ts(eng, out, in0, s1, op0, reads, writes, s2=None, op1=None):
            if op1 is None:
                P.op(eng, lambda e: e.tensor_scalar(out=out, in0=in0, scalar1=s1, scalar2=None, op0=op0),
                     reads, writes)
            else:
                P.op(eng, lambda e: e.tensor_scalar(out=out, in0=in0, scalar1=s1, scalar2=s2, op0=op0, op1=op1),
                     reads, writes)

        def stt(eng, out, in0, scalar, in1, op0, op1, reads, writes):
            P.op(eng, lambda e: e.scalar_tensor_tensor(out=out, in0=in0, scalar=scalar, in1=in1, op0=op0, op1=op1),
                 reads, writes)

        def memset(eng, ap, val, writes):
            P.op(eng, lambda e: e.memset(ap, val), (), writes)

        def dma(out, in_, reads, writes, key, eng="sp"):
            P.op(eng, lambda e: e.dma_start(out=out, in_=in_), reads, writes, dma=key)

        def load_wA(name, idx, width=4096):
            b = cntA[0] % 2
            cntA[0] += 1
            dma(wA[b][:, 0:width], wtile(name, idx), [R_wbf], [R_wA[b]], "dwA%d" % b)
            return wA[b], R_wA[b]

        def load_wB(name, idx, width):
            b = cntB[0] % 2
            cntB[0] += 1
            dma(wB[b][:, 0:width], wtile(name, idx), [R_wbf], [R_wB[b]], "dwB%d" % b)
            return wB[b], R_wB[b]

        R_wbf = Res()

        dma(pc_t[:], pcat_d.ap(), [], [R_pc], "d_pc")
        dma(bc_t[:], bcat_d.ap().partition_broadcast(128), [], [R_bc], "d_bc")
        memset("pool", c_mhalf[:], -0.5, [R_const])
        memset("pool", Vaug[:], 1.0, [R_V])
        memset("pool", carry[:], 0.0, [R_carry])
        memset("pool", kcT[:], 0.0, [R_kc])
        memset("pool", hidv[:], 0.0, [R_hidv])
        memset("pool", vcmp[:], 0.0, [R_vc])
        for b in range(31):
            tt("dve", tabd[:, b * 16:(b + 1) * 16], bc_t[:, b * 16:(b + 1) * 16], bc_t[:, 496:512], ALU.subtract,
               [R_bc], [R_const])
        reset_local()
        ohbuf = sb([128, 8192], F32, "ohbuf"); R_oh = Res()
        work = sb([128, 16, 256], F32, "work"); R_work = Res()
        dma(ohbuf[:], ohT_d.ap(), [], [R_oh], "d_oh")
        for h in range(16):
            tcopy("dve" if h % 2 == 0 else "pool", work[:, h, :], ohbuf[:, 31 * 256:32 * 256], [R_oh], [R_work])
        for b in range(31):
            for h in range(16):
                stt("dve", work[:, h, :], ohbuf[:, b * 256:(b + 1) * 256], tabd[:, b * 16 + h:b * 16 + h + 1],
                    work[:, h, :], ALU.mult, ALU.add, [R_oh, R_const, R_work], [R_work])
        tcopy("dve", MT[:], work[:], [R_work], [R_mast])
        dma(ohbuf[:, 0:4864], ohC_d.ap(), [], [R_oh], "d_oh")
        wv = work[:].rearrange("p h y -> p (h y)")
        for h in range(16):
            tcopy("dve" if h % 2 == 0 else "pool", wv[:, h * 152:(h + 1) * 152], ohbuf[:, 31 * 152:32 * 152],
                  [R_oh], [R_work])
        for b in range(31):
            for h in range(16):
                stt("dve", wv[:, h * 152:(h + 1) * 152], ohbuf[:, b * 152:(b + 1) * 152],
                    tabd[:, b * 16 + h:b * 16 + h + 1], wv[:, h * 152:(h + 1) * 152], ALU.mult, ALU.add,
                    [R_oh, R_const, R_work], [R_work])
        tcopy("dve", MC[:].rearrange("p h x -> p (h x)"), wv[:, 0:16 * 152], [R_work], [R_mast])
        P.barrier(with_sp=True)

        reset_local()
        npp = wtotal // 128
        nconv = npp // CONV_W
        NST = 3
        stg_in = [sb([128, CONV_W], F32, "cin") for _ in range(NST)]
        stg_out = [sb([128, CONV_W], BF16, "cout") for _ in range(NST)]
        R_ci = [Res() for _ in range(NST)]
        R_co = [Res() for _ in range(NST)]
        for i in range(nconv):
            s_ = i % NST
            src = bass.AP(wcat, i * CONV_W, [[npp, 128], [1, CONV_W]])
            dst = bass.AP(wbf, i * CONV_W, [[npp, 128], [1, CONV_W]])
            dma(stg_in[s_][:], src, [], [R_ci[s_]], "d_ci%d" % s_)
            if i % 2 == 0:
                tcopy("dve", stg_out[s_][:], stg_in[s_][:], [R_ci[s_]], [R_co[s_]])
            else:
                act(stg_out[s_][:], stg_in[s_][:], AF.Copy, [R_ci[s_]], [R_co[s_]])
            dma(dst, stg_out[s_][:], [R_co[s_]], [R_wbf], "d_co%d" % s_, eng="pool")
        P.barrier(with_sp=True)
        for (t_, nm) in ((c_ident, "ident"), (c_ones, "ones"), (c_tril, "tril"), (c_wmask, "wmask"),
                         (c_emast, "emast"), (c_ov, "ov")):
            dma(t_[:], wtile(nm, 0), [R_wbf], [R_const], "d_c_" + nm)
        dma(c_w2ck[:].rearrange("p a b -> p (a b)"), wtile("w2ck", 0), [R_wbf], [R_const], "d_c_w2ck")
        dma(c_w2cv[:].rearrange("p a b -> p (a b)"), wtile("w2cv", 0), [R_wbf], [R_const], "d_c_w2cv")
        dma(c_peT[:].rearrange("p a b -> p (a b)"), wtile("peT", 0), [R_wbf], [R_const], "d_c_peT")
        for s_ in range(2):
            for pc in range(2):
                wt, rw = load_wA("w1c", (s_ * 2 + pc) * 2)
                mm(banks[0][:, 0:1], [(wt[0:64, l * 128:(l + 1) * 128], c_peT[0:64, s_, l:l + 1]) for l in range(32)],
                   [rw, R_const], [Rb[0]])
                tt("dve", cbias[:, s_ * 2 + pc:s_ * 2 + pc + 1], banks[0][:, 0:1], lnp("b1T", s_ * 2 + pc), ALU.add,
                   [Rb[0], R_pc], [R_const])
        P.barrier(with_sp=True)

        def layer_norm(idx, sq, nmean, msq, tbuf):
            for dc in range(8):
                tcopy("dve", hb[:, dc, :], h32[:, dc, :], [R_h[dc]], [R_hb[dc]])
                act(sq[0][:, dc, :], h32[:, dc, :], AF.Square, [R_h[dc]], [sq[1]])
            mm(banks[6][:], [(c_ones[:], hb[:, dc, :]) for dc in range(8)], R_hb + [R_const], [Rb[6]])
            mm(banks[7][:], [(c_ones[:], sq[0][:, dc, :]) for dc in range(8)], [sq[1], R_const], [Rb[7]])
            ts("dve", nmean[0][:], banks[6][:], -1.0 / D, ALU.mult, [Rb[6]], [nmean[1]])
            tt("dve", msq[0][:], nmean[0][:], nmean[0][:], ALU.mult, [nmean[1]], [msq[1]])
            stt("dve", msq[0][:], banks[7][:], 1.0 / D, msq[0][:], ALU.mult, ALU.subtract, [Rb[7], msq[1]], [msq[1]])
            ts("dve", msq[0][:], msq[0][:], EPS_LN / (ALPHA * ALPHA), ALU.add, [msq[1]], [msq[1]])
            tt("pool", msq[0][:], msq[0][:], c_mhalf[:], ALU.pow, [msq[1], R_const], [msq[1]])
            for dc in range(8):
                tb = tbuf[dc % 2]
                tt("dve", tb[0][:], h32[:, dc, :], nmean[0][:], ALU.add, [R_h[dc], nmean[1]], [tb[1]])
                tt("dve", tb[0][:], tb[0][:], msq[0][:], ALU.mult, [tb[1], msq[1]], [tb[1]])
                act(h32[:, dc, :], tb[0][:], AF.Identity, [tb[1], R_pc], [R_h[dc]],
                    bias=lnp("lnB", idx * 8 + dc), scale=lnp("lnG", idx * 8 + dc))
                act(hb[:, dc, :], tb[0][:], AF.Identity, [tb[1], R_pc], [R_hb[dc]],
                    bias=lnp("lnB", idx * 8 + dc), scale=lnp("lnG", idx * 8 + dc))

        def ln_scratch():
            sq = (sb([128, 8, CH], BF16, "sq"), Res())
            nmean = (sb([128, CH], F32, "nmean"), Res())
            msq = (sb([128, CH], F32, "msq"), Res())
            tbuf = [(sb([128, CH], F32, "tb"), Res()) for _ in range(2)]
            return sq, nmean, msq, tbuf

        def ffn(l, j):
            reset_local()
            g = sb([128, NFC, CH], BF16, "g"); R_g = [Res() for _ in range(NFC)]
            sa = [(sb([128, CH], F32, "sa"), Res()) for _ in range(2)]
            lsc = ln_scratch()
            cres = 0.5 / ALPHA
            if do_ffn:
                for gi in range(11):
                    wt, rw = load_wA("w13_%d_%d" % (l, j), gi)
                    for fi in range(2):
                        fc = gi * 2 + fi
                        pa, ra = banks[fc % 2], Rb[fc % 2]
                        pb_, rb_ = banks[2 + fc % 2], Rb[2 + fc % 2]
                        mm(pa[:], [(wt[:, kc * 256 + fi * 128:kc * 256 + fi * 128 + 128], hb[:, kc, :]) for kc in range(8)],
                           [rw] + R_hb, [ra])
                        mm(pb_[:], [(wt[:, 2048 + kc * 256 + fi * 128:2048 + kc * 256 + fi * 128 + 128], hb[:, kc, :])
                                    for kc in range(8)], [rw] + R_hb, [rb_])
                        s_t, s_r = sa[fc % 2]
                        act(s_t[:], pa[:], AF.Silu, [ra], [s_r])
                        tt("dve", g[:, fc, :], s_t[:], pb_[:], ALU.mult, [s_r, rb_], [R_g[fc]])
                for dg in range(8):
                    wt, rw = load_wB("w2_%d_%d" % (l, j), dg, NFC * 128)
                    py, ry = banks[4 + dg % 2], Rb[4 + dg % 2]
                    mm(py[:], [(wt[:, fc * 128:(fc + 1) * 128], g[:, fc, :]) for fc in range(NFC)], [rw] + R_g, [ry])
                    stt("dve", h32[:, dg, :], py[:], cres, h32[:, dg, :], ALU.mult, ALU.add, [ry, R_h[dg]], [R_h[dg]])
            layer_norm(l * 3 + (0 if j == 0 else 2), *lsc)
            P.barrier()

        def sgu(l):
            a = l
            reset_local()
            u = sb([128, 16, CH], BF16, "u"); R_u = [Res() for _ in range(16)]
            vg_off = ptr[0]
            vg = sb([128, 4, 2048], F32, "vg"); R_vg = [Res() for _ in range(4)]
            vln = [(sb([128, 2048], BF16, "vln"), Res()) for _ in range(2)]
            addt = sb([128, 16, 128], F32, "addt"); R_addt = Res()
            stats = sb([128, 4, 24], F32, "stats"); R_stats = Res()
            mv = sb([128, 4, 2], F32, "mv"); R_mv = Res()
            rstd = sb([128, 4], F32, "rstd"); R_rstd = Res()
            tmpm = [(sb([128, 128], F32, "tmpm"), Res()) for _ in range(2)]
            dma(WT[:].rearrange("p g t -> p (g t)"), wtile("wsT_%d" % a, 0), [R_wbf], [R_WT], "d_WT")
            for gq in range(8):
                tt("dve", WT[:, gq, :], WT[:, gq, :], c_tril[:], ALU.mult, [R_WT, R_const], [R_WT])
            wtv = WT[:].rearrange("p g t -> p (g t)")
            mm(banks[6][:], [(c_ones[:], wtv[:, 0:512])], [R_WT, R_const], [Rb[6]])
            mm(banks[7][:], [(c_ones[:], wtv[:, 512:1024])], [R_WT, R_const], [Rb[7]])
            for cc in range(16):
                gq = cc // 2
                bk = banks[6 + gq // 4]
                stt("dve", addt[:, cc, :], bk[:, (gq % 4) * 128:(gq % 4) * 128 + 128], lnp("sgB", a * 16 + cc),
                    bc_t[:, 512 + (a * 8 + gq) * 128:512 + (a * 8 + gq) * 128 + 128], ALU.mult, ALU.add,
                    [Rb[6 + gq // 4], R_pc, R_bc], [R_addt])
            for gi in range(8):
                wt, rw = load_wA("win_%d" % a, gi, 2048)
                for fi in range(2):
                    cc = gi * 2 + fi
                    pu, ru = banks[cc % 2], Rb[cc % 2]
                    mm(pu[:], [(wt[:, kc * 256 + fi * 128:kc * 256 + fi * 128 + 128], hb[:, kc, :]) for kc in range(8)],
                       [rw] + R_hb, [ru])
                    act(u[:, cc, :], pu[:], AF.Gelu, [ru], [R_u[cc]])
            k = 0
            for gi in range(8):
                wt, rw = load_wA("win_%d" % a, 8 + gi, 2048)
                for t4 in range(4):
                    pv, rv = banks[2 + k % 2], Rb[2 + k % 2]
                    k += 1
                    mm(pv[:, 0:256], [(hb[:, kc, t4 * 128:(t4 + 1) * 128], wt[:, kc * 256:(kc + 1) * 256]) for kc in range(8)],
                       [rw] + R_hb, [rv])
                    act(vg[:, t4, gi * 256:(gi + 1) * 256], pv[:, 0:256], AF.Gelu, [rv], [R_vg[t4]])
            for t4 in range(4):
                for q in range(4):
                    P.op("dve", (lambda o, i: lambda e: e.bn_stats(out=o, in_=i))(stats[:, t4, q * 6:(q + 1) * 6],
                                                                               vg[:, t4, q * 512:(q + 1) * 512]),
                         [R_vg[t4]], [R_stats])
                P.op("dve", (lambda o, i: lambda e: e.bn_aggr(out=o, in_=i))(mv[:, t4, :], stats[:, t4, :]),
                     [R_stats], [R_mv])
            ts("dve", rstd[:, 0:4], mv[:, :, 1], EPS_LN, ALU.add, [R_mv], [R_rstd])
            tt("pool", rstd[:, 0:4], rstd[:, 0:4], c_mhalf[:, 0:4], ALU.pow, [R_rstd, R_const], [R_rstd])
            for t4 in range(4):
                vl, rvl = vln[t4 % 2]
                ts("dve", vl[:], vg[:, t4, :], mv[:, t4, 0:1], ALU.subtract, [R_vg[t4], R_mv, R_rstd], [rvl],
                   s2=rstd[:, t4:t4 + 1], op1=ALU.mult)
                for cq in range(4):
                    pm, rm = banks[4 + cq % 2], Rb[4 + cq % 2]
                    mm_multi([(pm[:, ci * 128:(ci + 1) * 128], vl[:, (cq * 4 + ci) * 128:(cq * 4 + ci + 1) * 128],
                               WT[:, (cq * 4 + ci) // 2, :], True, True, False) for ci in range(4)],
                             [rvl, R_WT], [rm])
                    for ci in range(4):
                        cc = cq * 4 + ci
                        tm, rtm = tmpm[ci % 2]
                        stt("dve", tm[:], pm[:, ci * 128:(ci + 1) * 128], lnp("sgG", a * 16 + cc), addt[:, cc, :],
                            ALU.mult, ALU.add, [rm, R_pc, R_addt], [rtm])
                        tt("dve", u[:, cc, t4 * 128:(t4 + 1) * 128], tm[:], u[:, cc, t4 * 128:(t4 + 1) * 128], ALU.mult,
                           [rtm, R_u[cc]], [R_u[cc]])
            for dg in range(8):
                wt, rw = load_wB("wout_%d" % a, dg, 16 * 128)
                py, ry = banks[dg % 2], Rb[dg % 2]
                mm(py[:], [(wt[:, cc * 128:(cc + 1) * 128], u[:, cc, :]) for cc in range(16)], [rw] + R_u, [ry])
                stt("dve", h32[:, dg, :], py[:], 1.0 / ALPHA, h32[:, dg, :], ALU.mult, ALU.add, [ry, R_h[dg]], [R_h[dg]])
            P.barrier()
            ptr[0] = vg_off
            lsc = ln_scratch()
            layer_norm(l * 3 + 1, *lsc)
            P.barrier()

        def kv_prep(c):
            reset_local()
            T0 = c * CH
            kv0 = sb([128, 4, 528], BF16, "kv0"); R_kv0 = Res()
            hid = sb([128, 2, 2, 4, 32], BF16, "hid"); R_hid = Res()
            vct = sb([128, 4, 64], BF16, "vct"); R_vct = Res()
            for gi in range(4):
                wt, rw = load_wA("kk", gi, 2048)
                for fi in range(2):
                    fch = gi * 2 + fi
                    pk, rk = banks[fch % 2], Rb[fch % 2]
                    mm(pk[:], [(wt[:, kc * 256 + fi * 128:kc * 256 + fi * 128 + 128], hb[:, kc, :]) for kc in range(8)],
                       [rw] + R_hb, [rk])
                    if fch % 2 == 0:
                        tcopy("dve", kTs[:, fch, T0:T0 + CH], pk[:], [rk], [R_kTs])
                    else:
                        act(kTs[:, fch, T0:T0 + CH], pk[:], AF.Copy, [rk], [R_kTs])
            wt, rw = load_wA("kvv", 0)
            for t4 in range(4):
                pv, rv = banks[2 + t4 % 2], Rb[2 + t4 % 2]
                mm(pv[:], [(hb[:, kc, t4 * 128:(t4 + 1) * 128], wt[:, kc * 512:(kc + 1) * 512]) for kc in range(8)],
                   [rw] + R_hb, [rv])
                tcopy("dve", Vaug[:, 4 * c + t4, :, 0:64], pv[:].rearrange("p (a d) -> p a d", d=64), [rv], [R_V])
            if stage < 0.1:
                P.barrier()
                return
            tcopy("dve", kv0[:, :, 0:16], carry[:], [R_carry], [R_kv0])
            wt, rw = load_wA("kvc", 0)
            for fch in range(4):
                pc_, rc_ = banks[4 + fch % 2], Rb[4 + fch % 2]
                mm(pc_[:], [(wt[:, kc * 512 + fch * 128:kc * 512 + fch * 128 + 128], hb[:, kc, :]) for kc in range(8)],
                   [rw] + R_hb, [rc_])
                act(kv0[:, fch, 16:528], pc_[:], AF.Copy, [rc_], [R_kv0])
            tcopy("dve", carry[:], kv0[:, :, 512:528], [R_kv0], [R_carry])
            if stage < 0.3:
                P.barrier()
                return
            for s_ in range(2):
                for pc in range(2):
                    ph, rh = banks[6 + (s_ * 2 + pc) % 2], Rb[6 + (s_ * 2 + pc) % 2]
                    for half in range(2):
                        wt, rw = load_wA("w1c", (s_ * 2 + pc) * 2 + half)
                        for gq in (half, half + 2):
                            fch = s_ * 2 + gq // 2
                            mm(ph[:, gq * 32:(gq + 1) * 32],
                               [(wt[:, l * 128:(l + 1) * 128], kv0[:, fch, l:l + 497:16]) for l in range(32)],
                               [rw, R_kv0], [rh])
                    if s_ == 0:
                        act(hid[:, s_, pc, :, :].rearrange("p g n -> p (g n)"), ph[:, 0:128], AF.Gelu, [rh, R_const], [R_hid],
                            bias=cbias[:, s_ * 2 + pc:s_ * 2 + pc + 1])
                    else:
                        act(hidv[:, pc, :, 32 * c:32 * c + 32], ph[:, 0:128].rearrange("p (g n) -> p g n", n=32), AF.Gelu,
                            [rh, R_const], [R_hidv], bias=cbias[:, s_ * 2 + pc:s_ * 2 + pc + 1])
            if stage < 0.5:
                P.barrier()
                return
            pk, rk = banks[0], Rb[0]
            for gq in range(4):
                mm(pk[:, gq * 32:(gq + 1) * 32], [(c_w2ck[:, pc, :], hid[:, 0, pc, gq, :]) for pc in range(2)],
                   [R_hid, R_const], [rk])
            tcopy("dve", kcT[:, :, 32 * c:32 * c + 32], pk[:, 0:128].rearrange("p (g n) -> p g n", n=32), [rk], [R_kc])
            if stage < 0.7:
                P.barrier()
                return
            NV = 32 * (c + 1)
            pv, rv = banks[1], Rb[1]
            for gq in range(4):
                mm(pv[:, gq * 64:(gq + 1) * 64], [(hidv[:, pc, gq, :], c_w2cv[:, pc, :]) for pc in range(2)],
                   [R_hidv, R_const], [rv])
            if stage != 0.8:
                if stage == 0.9:
                    tcopy("dve", kv0[:, 0, 0:256], pv[:, 0:256], [rv], [R_kv0])
                else:
                    act(vcmp[:].rearrange("p g d -> p (g d)"), pv[:, 0:256], AF.Copy, [rv], [R_vc])
            P.barrier()

        def nsa(l, c):
            bl = l - 2
            reset_local()
            T0 = c * CH
            NV = 32 * (c + 1)
            q_off = ptr[0]
            qT = sb([128, 8, 2, CH], BF16, "qT"); R_q = [Res() for _ in range(8)]
            gates = sb([128, 4, 48], F32, "gates"); R_gates = Res()
            snT = sb([128, 4, CH], BF16, "snT"); R_sn = [Res() for _ in range(4)]
            Pe = [(sb([128, 128], F32, "Pe"), Res()) for _ in range(2)]
            Pn = [(sb([128, 128], BF16, "Pn"), Res()) for _ in range(2)]
            PT = [(sb([128, 128], BF16, "PT"), Res()) for _ in range(2)]
            ET = [(sb([128, CH], BF16, "ET"), Res()) for _ in range(3)]
            small = sb([128, 64], F32, "small"); R_small = Res()
            imp_t = sb([128, 32], F32, "imp"); R_imp = Res()
            top8 = sb([128, 8], F32, "top8")
            snq = sb([128, 32], BF16, "snq"); R_snq = Res()
            o_acc = sb([128, 4, D], F32, "oacc"); R_oacc = [Res() for _ in range(4)]
            o_bf = sb([128, 4, D], BF16, "obf"); R_obf = [Res() for _ in range(4)]
            oTt = sb([128, 8, CH], BF16, "oT"); R_oT = [Res() for _ in range(8)]
            rdc = sb([128, 2, 8], F32, "rdc"); R_rdc = [Res(), Res()]
            for b_ in range(2):
                memset("dve", Pn[b_][0][:], 0.0, [Pn[b_][1]])
            for hc_ in range(8):
                memset("dve", qT[:, hc_, :, :], 0.0, [R_q[hc_]])
            if stage < 2:
                P.barrier()
                ptr[0] = q_off
                layer_norm(l * 3 + 1, *ln_scratch())
                P.barrier()
                return
            for gi in range(4):
                wt, rw = load_wA("wq_%d" % bl, gi, 2048)
                for fi in range(2):
                    hc = gi * 2 + fi
                    pq, rq = banks[hc % 2], Rb[hc % 2]
                    mm(pq[:], [(wt[:, kc * 256 + fi * 128:kc * 256 + fi * 128 + 128], hb[:, kc, :]) for kc in range(8)],
                       [rw] + R_hb, [rq])
                    act(qT[0:64, hc, 0, :], pq[0:64, :], AF.Copy, [rq], [R_q[hc]], scale=0.125)
                    act(qT[64:128, hc, 1, :], pq[64:128, :], AF.Copy, [rq], [R_q[hc]], scale=0.125)
            dma(wgt[:].rearrange("p a b -> p (a b)"), wtile("wg_%d" % bl, 0), [R_wbf], [R_wgt], "d_wgt")
            pg, rg = banks[2], Rb[2]
            for t4 in range(4):
                mm(pg[:, t4 * 48:(t4 + 1) * 48], [(hb[:, kc, t4 * 128:(t4 + 1) * 128], wgt[:, kc, :]) for kc in range(8)],
                   [R_wgt] + R_hb, [rg])
            act(gates[:].rearrange("p a b -> p (a b)"), pg[:, 0:192], AF.Sigmoid, [rg], [R_gates])
            if stage < 3:
                P.barrier()
                ptr[0] = q_off
                layer_norm(l * 3 + 1, *ln_scratch())
                P.barrier()
                return
            k = 0
            for qt in range(4):
                QT = 4 * c + qt
                xoff = 120 - 8 * QT
                ioff = 30 - 2 * QT
                for gq in range(4):
                    pimp, rimp = banks[3], Rb[3]
                    for r in range(4):
                        h = 4 * gq + r
                        hc = h // 2
                        r0 = (h % 2) * 64
                        psc, rsc = banks[k % 2], Rb[k % 2]
                        pe_t, pe_r = Pe[k % 2]
                        pn_t, pn_r = Pn[k % 2]
                        pt_t, pt_r = PT[k % 2]
                        k += 1
                        mm_multi([(psc[:, 0:NV], qT[:, hc, h % 2, qt * 128:(qt + 1) * 128], kcT[:, gq, 0:NV],
                                   True, False, False),
                                  (psc[:, 0:NV], c_ident[:], MC[:, h, xoff:xoff + NV], False, True, False)],
                                 [R_q[hc], R_kc, R_const, R_mast], [rsc])
                        sm = small[:, (k % 2) * 8:(k % 2) * 8 + 8]
                        P.op("dve", (lambda o, i: lambda e: e.reduce_max(out=o, in_=i, axis=AX.X))(sm[:, 0:1], psc[:, 1:NV]),
                             [rsc], [R_small])
                        ts("dve", sm[:, 1:2], sm[:, 0:1], -1.0, ALU.mult, [R_small], [R_small])
                        act(pe_t[:, 1:NV], psc[:, 1:NV], AF.Exp, [rsc, R_small], [pe_r, R_small], bias=sm[:, 1:2],
                            accum=sm[:, 2:3])
                        P.op("dve", (lambda o, i: lambda e: e.reciprocal(out=o, in_=i))(sm[:, 3:4], sm[:, 2:3]),
                             [R_small], [R_small])
                        if QT == 0:
                            tt("dve", sm[:, 3:4], sm[:, 3:4], lnp("rv0", 0), ALU.mult, [R_small, R_pc], [R_small])
                        ts("dve", pn_t[:, 1:NV], pe_t[:, 1:NV], sm[:, 3:4], ALU.mult, [pe_r, R_small], [pn_r])
                        ptp = banks[2][:, 256:320].bitcast(BF16)
                        P.op("pe", (lambda o, i: lambda e: e.transpose(o, i, c_ident[:]))(ptp[0:NV, :], pn_t[:, 0:NV]),
                             [pn_r, R_const], [Rb[2]])
                        tcopy("dve", pt_t[0:NV, :], ptp[0:NV, :], [Rb[2]], [pt_r])
                        poc = banks[2][:, 320 + (r % 2) * 64:320 + (r % 2) * 64 + 64]
                        mm(poc, [(pt_t[0:NV, :], vcmp[0:NV, gq, :])], [pt_r, R_vc], [Rb[2]])
                        mm(pimp[:, 0:32], [(pt_t[0:NV, :], c_ov[0:NV, :])], [pt_r, R_const], [rimp],
                           start=(r == 0), stop=(r == 3))
                        ts("dve", o_acc[:, qt, h * 64:(h + 1) * 64], poc, gates[:, qt, h * 3:h * 3 + 1], ALU.mult,
                           [Rb[2], R_gates], [R_oacc[qt]])
                    o_vm, _ = pcols["VM"]
                    o_fb, _ = pcols["FB"]
                    tt("dve", imp_t[:], pimp[:, 0:32], pc_t[:, o_vm + ioff:o_vm + ioff + 32], ALU.mult, [rimp, R_pc], [R_imp])
                    tt("dve", imp_t[:], imp_t[:], pc_t[:, o_fb + ioff:o_fb + ioff + 32], ALU.add, [R_imp, R_pc], [R_imp])
                    memset("dve", imp_t[:, 0:1], BIG, [R_imp])
                    P.op("dve", (lambda o, i: lambda e: e.max(out=o, in_=i))(top8[:], imp_t[:]), [R_imp], [R_small])
                    ts("dve", imp_t[:], imp_t[:], top8[:, 7:8], ALU.is_lt, [R_imp, R_small], [R_imp])
                    ts("dve", snq[:], imp_t[:], NEGM, ALU.mult, [R_imp], [R_snq])
                    pst = banks[2][:, 448:512].bitcast(BF16)
                    P.op("pe", (lambda o, i: lambda e: e.transpose(o, i, c_ident[:]))(pst[0:32, :], snq[:]),
                         [R_snq, R_const], [Rb[2]])
                    tcopy("dve", snT[0:32, gq, qt * 128:(qt + 1) * 128], pst[0:32, :], [Rb[2]], [R_sn[gq]])
            if stage < 4 and stage not in (3.2, 3.4, 3.6):
                P.barrier()
                ptr[0] = q_off
                layer_norm(l * 3 + 1, *ln_scratch())
                P.barrier()
                return
            o_c31 = 496
            kst = 0
            ke = 0
            for h in range(16):
                gq = h // 4
                hc = h // 2
                r0 = (h % 2) * 64
                for br in range(2):
                    if (stage == 3.2 and br == 1) or (stage == 3.4 and br == 0):
                        continue
                    pO, rO = banks[6 + br], Rb[6 + br]
                    P.op("dve", (lambda o: lambda e: e.memset(o, 0.0))(pO[:, 0:260]), [], [rO])
                    kt_lo = 0 if br == 0 else max(0, 4 * c - 4)
                    for kt in range(kt_lo, 4 * c + 4):
                        bs = [b for b in range(4) if (4 * c + b - kt) >= 0 and (br == 0 or (4 * c + b - kt) <= 4)]
                        if not bs:
                            continue
                        b_lo, b_hi = bs[0], bs[-1]
                        c0, c1 = b_lo * 128, (b_hi + 1) * 128
                        pst_, rst_ = banks[4 + kst % 2], Rb[4 + kst % 2]
                        kst += 1
                        items = [(pst_[:, c0:c1], kTs[:, br * 4 + gq, kt * 128:(kt + 1) * 128],
                                  qT[:, hc, h % 2, c0:c1], True, False, False)]
                        reads = [R_kTs, R_q[hc], R_const, R_mast]
                        if br == 0:
                            items.append((pst_[:, c0:c1], c_emast[0:32, kt * 128:(kt + 1) * 128], snT[0:32, gq, c0:c1],
                                          False, False, False))
                            reads.append(R_sn[gq])
                        for b in bs:
                            d = 4 * c + b - kt
                            if d == 0:
                                items.append((pst_[:, b * 128:(b + 1) * 128], c_ident[:], MT[:, h, 0:128], False, False, False))
                            elif d == 1:
                                items.append((pst_[:, b * 128:(b + 1) * 128], c_ident[:], MT[:, h, 128:256], False, False, False))
                            elif d == 4 and br == 1 and stage != 3.6:
                                items.append((pst_[:, b * 128:(b + 1) * 128], c_ident[:], c_wmask[:], False, False, False))
                        it = items[-1]
                        items[-1] = (it[0], it[1], it[2], it[3], True, it[5])
                        mm_multi(items, reads, [rst_])
                        et_t, et_r = ET[ke % 3]
                        ke += 1
                        act(et_t[:, c0:c1], pst_[:, c0:c1], AF.Exp, [rst_, R_bc], [et_r], bias=bc_t[:, o_c31 + h:o_c31 + h + 1])
                        mm_multi([(pO[:, b * 65:(b + 1) * 65], et_t[:, b * 128:(b + 1) * 128], Vaug[:, kt, br * 4 + gq, :],
                                   False, False, True) for b in bs], [et_r, R_V], [rO])
                    rd_t = rdc[:, br, :]
                    dview = pO[:, 0:260].rearrange("p (b x) -> p b x", x=65)
                    P.op("dve", (lambda o, i: lambda e: e.reciprocal(out=o, in_=i))(rd_t[:, 0:4], dview[:, :, 64]),
                         [rO], [R_rdc[br]])
                    tt("dve", rd_t[:, 4:8], rd_t[:, 0:4], gates[:, :, h * 3 + 1 + br], ALU.mult, [R_rdc[br], R_gates], [R_rdc[br]])
                    for b in range(4):
                        stt("dve", o_acc[:, b, h * 64:(h + 1) * 64], pO[:, b * 65:b * 65 + 64], rd_t[:, 4 + b:5 + b],
                            o_acc[:, b, h * 64:(h + 1) * 64], ALU.mult, ALU.add, [rO, R_rdc[br], R_oacc[b]], [R_oacc[b]])
            if stage < 5 and stage not in (3.2, 3.4, 3.6):
                P.barrier()
                ptr[0] = q_off
                layer_norm(l * 3 + 1, *ln_scratch())
                P.barrier()
                return
            for b in range(4):
                if b % 2 == 0:
                    tcopy("dve", o_bf[:, b, :], o_acc[:, b, :], [R_oacc[b]], [R_obf[b]])
                else:
                    act(o_bf[:, b, :], o_acc[:, b, :], AF.Copy, [R_oacc[b]], [R_obf[b]])
            for hc in range(8):
                ptr_, rtr_ = banks[hc % 2], Rb[hc % 2]
                ptv = ptr_[:, 0:256].bitcast(BF16)
                for b in range(4):
                    P.op("pe", (lambda o, i: lambda e: e.transpose(o, i, c_ident[:]))(
                        ptv[:, b * 128:(b + 1) * 128], o_bf[:, b, hc * 128:(hc + 1) * 128]), [R_obf[b], R_const], [rtr_])
                if hc % 2 == 0:
                    tcopy("dve", oTt[:, hc, :], ptv[:, 0:512], [rtr_], [R_oT[hc]])
                else:
                    act(oTt[:, hc, :], ptv[:, 0:512], AF.Copy, [rtr_], [R_oT[hc]])
            for gi in range(4):
                wt, rw = load_wA("wo_%d" % bl, gi, 2048)
                for fi in range(2):
                    dc = gi * 2 + fi
                    py, ry = banks[2 + dc % 2], Rb[2 + dc % 2]
                    mm(py[:], [(wt[:, kc * 256 + fi * 128:kc * 256 + fi * 128 + 128], oTt[:, kc, :]) for kc in range(8)],
                       [rw] + R_oT, [ry])
                    stt("dve", h32[:, dc, :], py[:], 1.0 / ALPHA, h32[:, dc, :], ALU.mult, ALU.add, [ry, R_h[dc]], [R_h[dc]])
            P.barrier()
            ptr[0] = q_off
            layer_norm(l * 3 + 1, *ln_scratch())
            P.barrier()

        R_out = Res()
        for s_ in range(nseq):
            for c in range(nchunk):
                T0 = c * CH
                if c == 0:
                    memset("pool", carry[:], 0.0, [R_carry])
                src = xT.ap()[s_].rearrange("(dc p) t -> p dc t", p=128)[:, :, T0:T0 + CH]
                dma(h32[:], src, [], R_h, "d_x")
                for dc in range(8):
                    tcopy("dve", hb[:, dc, :], h32[:, dc, :], [R_h[dc]], [R_hb[dc]])
                for l in layers:
                    if l == 2 and do_mixer:
                        kv_prep(c)
                    ffn(l, 0)
                    if do_mixer:
                        if l < 2:
                            sgu(l)
                        else:
                            nsa(l, c)
                    else:
                        reset_local()
                        lsc = ln_scratch()
                        layer_norm(l * 3 + 1, *lsc)
                        P.barrier()
                    ffn(l, 1)
                dst = oT.ap()[s_].rearrange("(dc p) t -> p dc t", p=128)[:, :, T0:T0 + CH]
                dma(dst, h32[:], R_h, [R_out], "d_out")
        P.final_wait("sp")
        P.replay(st)
    return nc


_CFG = {}


def kernel(**inputs):
    inp = {k: np.asarray(v) for k, v in inputs.items()}
    x = inp["x"].astype(np.float32, copy=False)
    wc = _wlayout(inp)
    wflat = wc.finish()
    pcat, pcols, bcat, ohT, ohC = _small_params(inp)
    nc = build_nc(wc.off, wflat.size, pcols, pcat.shape[1], _CFG)
    in_maps = []
    ncores = _CFG.get("ncores", NCORES)
    for core in range(ncores):
        xs = np.ascontiguousarray(x[core * BPC:(core + 1) * BPC].transpose(0, 2, 1))
        in_maps.append({"xT": xs, "wcat": wflat, "pcat": pcat, "bcat": bcat, "ohT": ohT, "ohC": ohC})
    res = run_bass_kernel_spmd(nc, in_maps, core_ids=list(range(ncores)))
    out = np.zeros((NCORES * BPC, S, D), np.float32)
    for core in range(ncores):
        o = np.asarray(res.results[core]["oT"])
        out[core * BPC:(core + 1) * BPC] = o.transpose(0, 2, 1)
    return out
```

```python
import math
from contextlib import ExitStack
import numpy as np
import concourse.bass as bass
import concourse.mybir as mybir
from concourse.bass_utils import run_bass_kernel_spmd

F32 = mybir.dt.float32
BF16 = mybir.dt.bfloat16
AF = mybir.ActivationFunctionType
ALU = mybir.AluOpType
AX = mybir.AxisListType

D = 1024
S = 2048
DEPTH = 4
FF = 2816
NFC = 22
CH = 512
NCH = S // CH
NCORES = 8
BPC = 4
ALPHA = (2.0 * DEPTH) ** 0.25
EPS_LN = 1e-5
NEGM = -32768.0
BIG = 1e30
SB_BASE = 16512
SB_LIMIT = 229344
CONV_W = 2048

ENGS = ("pe", "act", "dve", "pool", "sp")
CENGS = ("pe", "act", "dve", "pool")


class Res:
    __slots__ = ("w", "rs")

    def __init__(self):
        self.w = None
        self.rs = {}


class Prog:
    def __init__(self, nc):
        self.nc = nc
        self.ops = {e: [] for e in ENGS}
        self.cnt = {}
        self.waited = {e: {} for e in ENGS}

    def op(self, eng, fn, reads=(), writes=(), dma=None):
        deps = {}
        for r in reads:
            if r.w is not None:
                k, v = r.w
                if deps.get(k, 0) < v:
                    deps[k] = v
        for w in writes:
            if w.w is not None:
                k, v = w.w
                if deps.get(k, 0) < v:
                    deps[k] = v
            for k, v in w.rs.items():
                if deps.get(k, 0) < v:
                    deps[k] = v
        waits = []
        wd = self.waited[eng]
        for k, v in deps.items():
            if k == eng and eng == "pe":
                continue
            if wd.get(k, 0) >= v:
                continue
            wd[k] = v
            waits.append((k, v))
        if dma is not None:
            key, inc = dma, 16
        else:
            key, inc = eng, 1
        v = self.cnt.get(key, 0) + inc
        self.cnt[key] = v
        self.ops[eng].append((waits, fn, key, inc))
        for r in reads:
            if r.rs.get(key, 0) < v:
                r.rs[key] = v
        for w in writes:
            w.w = (key, v)
            w.rs = {}

    def barrier(self, with_sp=False):
        snap = dict(self.cnt)
        for e in (ENGS if with_sp else CENGS):
            waits = []
            wd = self.waited[e]
            for k, v in snap.items():
                if k == e and e == "pe":
                    continue
                if wd.get(k, 0) >= v:
                    continue
                wd[k] = v
                waits.append((k, v))
            if waits:
                self.ops[e].append((waits, None, None, 0))

    def final_wait(self, eng):
        snap = dict(self.cnt)
        self.ops[eng].append((list(snap.items()), None, None, 0))

    def replay(self, stack):
        nc = self.nc
        sems = {}
        for k in self.cnt.keys():
            sems[k] = stack.enter_context(nc.semaphore("s_" + str(k)))
        block = stack.enter_context(nc.Block())

        def run(engine, lst):
            for waits, fn, sk, inc in lst:
                for k, v in waits:
                    engine.wait_ge(sems[k], v)
                if fn is not None:
                    fn(engine).then_inc(sems[sk], inc)

        @block.tensor
        def _(e):
            run(e, self.ops["pe"])

        @block.scalar
        def _(e):
            run(e, self.ops["act"])

        @block.vector
        def _(e):
            run(e, self.ops["dve"])

        @block.gpsimd
        def _(e):
            run(e, self.ops["pool"])

        @block.sync
        def _(e):
            run(e, self.ops["sp"])


def _rel_bucket(dist):
    n = np.maximum(dist, 0)
    nf = np.maximum(n, 1).astype(np.float32)
    large = 16 + (np.log(nf / np.float32(16)) / np.float32(math.log(128 / 16)) * np.float32(16)).astype(np.int32)
    return np.where(n < 16, n, np.minimum(large, 31)).astype(np.int64)


class WCat:
    def __init__(self):
        self.parts = []
        self.off = {}
        self.n = 0

    def add(self, name, arr):
        arr = np.ascontiguousarray(arr, dtype=np.float32)
        assert arr.shape[1] == 128, (name, arr.shape)
        x = int(np.prod(arr.shape[2:]))
        self.off[name] = (self.n, x, arr.shape[0])
        self.parts.append(arr.reshape(-1))
        self.n += arr.size

    def finish(self):
        blk = 128 * CONV_W
        pad = (-self.n) % blk
        if pad:
            self.parts.append(np.zeros(pad, np.float32))
            self.n += pad
        return np.concatenate(self.parts)


def _wlayout(inp):
    wc = WCat()
    for l in range(DEPTH):
        for j in range(2):
            w1 = inp["ffn_w1"][l, j].reshape(8, 128, 11, 256).transpose(2, 1, 0, 3)
            w3 = inp["ffn_w3"][l, j].reshape(8, 128, 11, 256).transpose(2, 1, 0, 3)
            w13 = np.concatenate([w1.reshape(11, 128, 2048), w3.reshape(11, 128, 2048)], axis=2)
            wc.add("w13_%d_%d" % (l, j), w13)
            wc.add("w2_%d_%d" % (l, j), inp["ffn_w2"][l, j].reshape(22, 128, 8, 128).transpose(2, 1, 0, 3))
    for a in range(2):
        wc.add("win_%d" % a, inp["sgu_w_in"][a].reshape(8, 128, 16, 256).transpose(2, 1, 0, 3))
        wc.add("wout_%d" % a, inp["sgu_w_out"][a].reshape(16, 128, 8, 128).transpose(2, 1, 0, 3))
        wc.add("wsT_%d" % a, inp["sgu_w_s"][a].transpose(2, 0, 1)[None])
    kv = inp["kv_w"].reshape(D, 6, 4, 64)
    kk = np.stack([kv[:, 2], kv[:, 4]], axis=1)
    kk = np.repeat(kk[:, :, :, None, :], 2, axis=3).reshape(D, 1024)
    wc.add("kk", kk.reshape(8, 128, 4, 256).transpose(2, 1, 0, 3))
    kvv = np.stack([kv[:, 3], kv[:, 5]], axis=1).reshape(D, 512)
    wc.add("kvv", kvv.reshape(8, 128, 512).transpose(1, 0, 2)[None])
    kvc = np.stack([kv[:, 0], kv[:, 1]], axis=1).reshape(D, 512)
    wc.add("kvc", kvc.reshape(8, 128, 512).transpose(1, 0, 2)[None])
    w1c = inp["cmp_w1"].reshape(2, 32, 64, 2, 128).transpose(0, 3, 2, 1, 4)
    zz = np.zeros_like(w1c)
    w1c = np.stack([np.concatenate([w1c, zz], axis=2), np.concatenate([zz, w1c], axis=2)], axis=2)
    wc.add("w1c", w1c.reshape(8, 128, 32, 128))
    w2k = inp["cmp_w2"][0].reshape(2, 128, 64).transpose(1, 0, 2)
    wc.add("w2ck", np.concatenate([w2k, w2k], axis=2)[None])
    wc.add("w2cv", inp["cmp_w2"][1].reshape(2, 128, 64).transpose(1, 0, 2)[None])
    peT = inp["cmp_pe"].transpose(2, 0, 1)
    wc.add("peT", np.concatenate([peT, peT], axis=0)[None])
    for bl in range(2):
        wq = inp["nsa_w_qg"][bl][:, :1024]
        wc.add("wq_%d" % bl, wq.reshape(8, 128, 4, 256).transpose(2, 1, 0, 3))
        wg = inp["nsa_w_qg"][bl][:, 1024:]
        wc.add("wg_%d" % bl, wg.reshape(8, 128, 48).transpose(1, 0, 2)[None])
        wc.add("wo_%d" % bl, inp["nsa_w_o"][bl].reshape(8, 128, 4, 256).transpose(2, 1, 0, 3))
    ii = np.arange(128)
    wc.add("ident", np.eye(128, dtype=np.float32)[None])
    wc.add("ones", np.ones((1, 128, 128), np.float32))
    wc.add("tril", (ii[None, :] >= ii[:, None]).astype(np.float32)[None])
    wc.add("wmask", np.where(ii[None, :] >= ii[:, None], NEGM, 0.0).astype(np.float32)[None])
    em = np.zeros((128, 2048), np.float32)
    for j in range(32):
        em[j, 64 * j:64 * j + 64] = 1.0
    wc.add("emast", em[None])
    ov = np.zeros((128, 32), np.float32)
    for m in range(1, 128):
        for j in range(32):
            if 4 * j <= m <= 4 * j + 4:
                ov[m, j] = 1.0
    wc.add("ov", ov[None])
    return wc


def _small_params(inp):
    cols = {}
    parts = []
    n = 0

    def add(name, a):
        nonlocal n
        a = np.ascontiguousarray(a, np.float32).reshape(128, -1)
        cols[name] = (n, a.shape[1])
        parts.append(a)
        n += a.shape[1]

    add("lnG", inp["ln_g"].reshape(12, 8, 128).transpose(2, 0, 1))
    add("lnB", inp["ln_b"].reshape(12, 8, 128).transpose(2, 0, 1))
    add("sgG", inp["sgu_ln_g"].reshape(2, 16, 128).transpose(2, 0, 1))
    add("sgB", inp["sgu_ln_b"].reshape(2, 16, 128).transpose(2, 0, 1))
    add("b1T", inp["cmp_b1"].reshape(2, 2, 128).transpose(2, 0, 1))
    ii = np.arange(128)
    add("rv0", (ii >= 31).astype(np.float32)[:, None])
    add("epsln", np.full((128, 1), EPS_LN / (ALPHA * ALPHA), np.float32))
    add("epssg", np.full((128, 1), EPS_LN, np.float32))
    ci = (ii >= 64).astype(np.int64)[:, None]
    x = np.arange(-30, 32)[None, :]
    forced = (x == ci - 1) | (x == ci)
    invalid = x > ci
    vm = (~forced & ~invalid).astype(np.float32)
    fb = np.where(forced, BIG, np.where(invalid, -BIG, 0.0)).astype(np.float32)
    add("VM", vm)
    add("FB", fb)
    pcat = np.concatenate(parts, axis=1)
    bcat = np.concatenate([inp["rel_table"].reshape(-1), inp["sgu_b_s"].reshape(-1)]).astype(np.float32)[None]
    j = ii[:, None]
    y = np.arange(256)[None, :]
    dT = y - j
    bT = _rel_bucket(dT)
    oht = np.stack([((bT == b) & (dT >= 0)).astype(np.float32) for b in range(31)], axis=1)
    negt = np.where(dT < 0, NEGM, 0.0).astype(np.float32)
    xx = np.arange(-120, 32)[None, :]
    dC = j - 16 * xx - 15
    bC = _rel_bucket(dC)
    ohc = np.stack([((bC == b) & (dC >= 0)).astype(np.float32) for b in range(31)], axis=1)
    negc = np.where(dC < 0, NEGM, 0.0).astype(np.float32)
    ohT = np.concatenate([oht.reshape(128, -1), negt], axis=1)
    ohC = np.concatenate([ohc.reshape(128, -1), negc], axis=1)
    return pcat, cols, bcat, np.ascontiguousarray(ohT), np.ascontiguousarray(ohC)


DTSIZE = {F32: 4, BF16: 2}


def build_nc(woff, wtotal, pcols, npcat, cfg):
    nseq = cfg.get("nseq", BPC)
    nchunk = cfg.get("nchunk", NCH)
    layers = cfg.get("layers", list(range(DEPTH)))
    do_mixer = cfg.get("mixer", True)
    do_ffn = cfg.get("ffn", True)
    stage = cfg.get("nsa_stage", 9)

    nc = bass.Bass("TRN2", target_bir_lowering=False)
    xT = nc.dram_tensor("xT", [BPC, D, S], F32, kind="ExternalInput")
    wcat = nc.dram_tensor("wcat", [wtotal], F32, kind="ExternalInput")
    pcat_d = nc.dram_tensor("pcat", [128, npcat], F32, kind="ExternalInput")
    bcat_d = nc.dram_tensor("bcat", [1, 2560], F32, kind="ExternalInput")
    ohT_d = nc.dram_tensor("ohT", [128, 8192], F32, kind="ExternalInput")
    ohC_d = nc.dram_tensor("ohC", [128, 4864], F32, kind="ExternalInput")
    oT = nc.dram_tensor("oT", [BPC, D, S], F32, kind="ExternalOutput")
    wbf = nc.dram_tensor("wbf", [wtotal], BF16)

    P = Prog(nc)
    uid = [0]
    ptr = [SB_BASE]

    def sb(shape, dt, name="t"):
        uid[0] += 1
        size = int(np.prod(shape[1:])) * DTSIZE[dt]
        size = (size + 31) // 32 * 32
        off = ptr[0]
        assert off + size <= SB_LIMIT, ("SBUF overflow", name, off, size)
        ptr[0] = off + size
        return nc.alloc_sbuf_tensor_at("%s_%d" % (name, uid[0]), list(shape), dt, offset=off)

    def wtile(name, idx):
        off, x, nt = woff[name]
        assert idx < nt
        return bass.AP(wbf, off + idx * 128 * x, [[x, 128], [1, x]])

    with ExitStack() as st:
        banks = [st.enter_context(nc.psum_tensor("bank%d" % i, [128, 512], F32)) for i in range(8)]
        Rb = [Res() for _ in range(8)]

        pc_t = sb([128, npcat], F32, "pcat"); R_pc = Res()
        bc_t = sb([128, 2560], F32, "bcat"); R_bc = Res()
        c_ident = sb([128, 128], BF16, "ident")
        c_ones = sb([128, 128], BF16, "ones")
        c_tril = sb([128, 128], BF16, "tril")
        c_wmask = sb([128, 128], BF16, "wmask")
        c_emast = sb([128, 2048], BF16, "emast")
        c_ov = sb([128, 32], BF16, "ov")
        c_w2ck = sb([128, 2, 128], BF16, "w2ck")
        c_w2cv = sb([128, 2, 64], BF16, "w2cv")
        c_peT = sb([128, 2, 32], BF16, "peT")
        c_mhalf = sb([128, 512], F32, "mhalf")
        tabd = sb([128, 512], F32, "tabd")
        cbias = sb([128, 4], F32, "cbias")
        R_const = Res()
        MT = sb([128, 16, 256], BF16, "MT")
        MC = sb([128, 16, 152], BF16, "MC")
        R_mast = Res()
        kTs = sb([128, 8, S], BF16, "kTs"); R_kTs = Res()
        Vaug = sb([128, 16, 8, 65], BF16, "Vaug"); R_V = Res()
        kcT = sb([128, 4, 128], BF16, "kcT"); R_kc = Res()
        hidv = sb([128, 2, 4, 128], BF16, "hidv"); R_hidv = Res()
        vcmp = sb([128, 4, 64], BF16, "vcmp"); R_vc = Res()
        carry = sb([128, 4, 16], BF16, "carry"); R_carry = Res()
        h32 = sb([128, 8, CH], F32, "h32"); R_h = [Res() for _ in range(8)]
        hb = sb([128, 8, CH], BF16, "hb"); R_hb = [Res() for _ in range(8)]
        wA = [sb([128, 4096], BF16, "wA") for _ in range(2)]; R_wA = [Res(), Res()]
        wB = [sb([128, 22 * 128], BF16, "wB") for _ in range(2)]; R_wB = [Res(), Res()]
        WT = sb([128, 8, 128], BF16, "WT"); R_WT = Res()
        wgt = sb([128, 8, 48], BF16, "wgt"); R_wgt = Res()
        local_base = ptr[0]
        cntA = [0]
        cntB = [0]

        def lnp(name, idx):
            o, n = pcols[name]
            return pc_t[:, o + idx:o + idx + 1]

        def pslice(name):
            o, n = pcols[name]
            return pc_t[:, o:o + n]

        def reset_local():
            ptr[0] = local_base

        def mm(out_ap, pairs, reads, writes, start=True, stop=True, skip=False):
            pairs = list(pairs)

            def fn(e):
                n = len(pairs)
                ins = None
                for i, (l, r) in enumerate(pairs):
                    ins = e.matmul(out_ap, l, r, start=(start and i == 0), stop=(stop and i == n - 1),
                                   skip_group_check=skip)
                return ins
            P.op("pe", fn, reads, writes)

        def mm_multi(items, reads, writes):
            items = list(items)

            def fn(e):
                ins = None
                for (o, l, r, s0, s1, sk) in items:
                    ins = e.matmul(o, l, r, start=s0, stop=s1, skip_group_check=sk)
                return ins
            P.op("pe", fn, reads, writes)

        def act(out, in_, func, reads, writes, bias=0.0, scale=1.0, accum=None):
            P.op("act", lambda e: e.activation(out=out, in_=in_, func=func, bias=bias, scale=scale,
                                               accum_out=accum), reads, writes)

        def tcopy(eng, out, in_, reads, writes):
            P.op(eng, lambda e: e.tensor_copy(out=out, in_=in_), reads, writes)

        def tt(eng, out, in0, in1, op, reads, writes):
            P.op(eng, lambda e: e.tensor_tensor(out=out, in0=in0, in1=in1, op=op), reads, writes)

        def ts(eng, out, in0, s1, op0, reads, writes, s2=None, op1=None):
            if op1 is None:
                P.op(eng, lambda e: e.tensor_scalar(out=out, in0=in0, scalar1=s1, scalar2=None, op0=op0),
                     reads, writes)
            else:
                P.op(eng, lambda e: e.tensor_scalar(out=out, in0=in0, scalar1=s1, scalar2=s2, op0=op0, op1=op1),
                     reads, writes)

        def stt(eng, out, in0, scalar, in1, op0, op1, reads, writes):
            P.op(eng, lambda e: e.scalar_tensor_tensor(out=out, in0=in0, scalar=scalar, in1=in1, op0=op0, op1=op1),
                 reads, writes)

        def memset(eng, ap, val, writes):
            P.op(eng, lambda e: e.memset(ap, val), (), writes)

        def dma(out, in_, reads, writes, key, eng="sp"):
            P.op(eng, lambda e: e.dma_start(out=out, in_=in_), reads, writes, dma=key)

        def load_wA(name, idx, width=4096):
            b = cntA[0] % 2
            cntA[0] += 1
            dma(wA[b][:, 0:width], wtile(name, idx), [R_wbf], [R_wA[b]], "dwA%d" % b)
            return wA[b], R_wA[b]

        def load_wB(name, idx, width):
            b = cntB[0] % 2
            cntB[0] += 1
            dma(wB[b][:, 0:width], wtile(name, idx), [R_wbf], [R_wB[b]], "dwB%d" % b)
            return wB[b], R_wB[b]

        R_wbf = Res()

        dma(pc_t[:], pcat_d.ap(), [], [R_pc], "d_pc")
        dma(bc_t[:], bcat_d.ap().partition_broadcast(128), [], [R_bc], "d_bc")
        memset("pool", c_mhalf[:], -0.5, [R_const])
        memset("pool", Vaug[:], 1.0, [R_V])
        memset("pool", carry[:], 0.0, [R_carry])
        memset("pool", kcT[:], 0.0, [R_kc])
        memset("pool", hidv[:], 0.0, [R_hidv])
        memset("pool", vcmp[:], 0.0, [R_vc])
        for b in range(31):
            tt("dve", tabd[:, b * 16:(b + 1) * 16], bc_t[:, b * 16:(b + 1) * 16], bc_t[:, 496:512], ALU.subtract,
               [R_bc], [R_const])
        reset_local()
        ohbuf = sb([128, 8192], F32, "ohbuf"); R_oh = Res()
        work = sb([128, 16, 256], F32, "work"); R_work = Res()
        dma(ohbuf[:], ohT_d.ap(), [], [R_oh], "d_oh")
        for h in range(16):
            tcopy("dve" if h % 2 == 0 else "pool", work[:, h, :], ohbuf[:, 31 * 256:32 * 256], [R_oh], [R_work])
        for b in range(31):
            for h in range(16):
                stt("dve", work[:, h, :], ohbuf[:, b * 256:(b + 1) * 256], tabd[:, b * 16 + h:b * 16 + h + 1],
                    work[:, h, :], ALU.mult, ALU.add, [R_oh, R_const, R_work], [R_work])
        tcopy("dve", MT[:], work[:], [R_work], [R_mast])
        dma(ohbuf[:, 0:4864], ohC_d.ap(), [], [R_oh], "d_oh")
        wv = work[:].rearrange("p h y -> p (h y)")
        for h in range(16):
            tcopy("dve" if h % 2 == 0 else "pool", wv[:, h * 152:(h + 1) * 152], ohbuf[:, 31 * 152:32 * 152],
                  [R_oh], [R_work])
        for b in range(31):
            for h in range(16):
                stt("dve", wv[:, h * 152:(h + 1) * 152], ohbuf[:, b * 152:(b + 1) * 152],
                    tabd[:, b * 16 + h:b * 16 + h + 1], wv[:, h * 152:(h + 1) * 152], ALU.mult, ALU.add,
                    [R_oh, R_const, R_work], [R_work])
        tcopy("dve", MC[:].rearrange("p h x -> p (h x)"), wv[:, 0:16 * 152], [R_work], [R_mast])
        P.barrier(with_sp=True)

        reset_local()
        npp = wtotal // 128
        nconv = npp // CONV_W
        NST = 3
        stg_in = [sb([128, CONV_W], F32, "cin") for _ in range(NST)]
        stg_out = [sb([128, CONV_W], BF16, "cout") for _ in range(NST)]
        R_ci = [Res() for _ in range(NST)]
        R_co = [Res() for _ in range(NST)]
        for i in range(nconv):
            s_ = i % NST
            src = bass.AP(wcat, i * CONV_W, [[npp, 128], [1, CONV_W]])
            dst = bass.AP(wbf, i * CONV_W, [[npp, 128], [1, CONV_W]])
            dma(stg_in[s_][:], src, [], [R_ci[s_]], "d_ci%d" % s_)
            if i % 2 == 0:
                tcopy("dve", stg_out[s_][:], stg_in[s_][:], [R_ci[s_]], [R_co[s_]])
            else:
                act(stg_out[s_][:], stg_in[s_][:], AF.Copy, [R_ci[s_]], [R_co[s_]])
            dma(dst, stg_out[s_][:], [R_co[s_]], [R_wbf], "d_co%d" % s_, eng="pool")
        P.barrier(with_sp=True)
        for (t_, nm) in ((c_ident, "ident"), (c_ones, "ones"), (c_tril, "tril"), (c_wmask, "wmask"),
                         (c_emast, "emast"), (c_ov, "ov")):
            dma(t_[:], wtile(nm, 0), [R_wbf], [R_const], "d_c_" + nm)
        dma(c_w2ck[:].rearrange("p a b -> p (a b)"), wtile("w2ck", 0), [R_wbf], [R_const], "d_c_w2ck")
        dma(c_w2cv[:].rearrange("p a b -> p (a b)"), wtile("w2cv", 0), [R_wbf], [R_const], "d_c_w2cv")
        dma(c_peT[:].rearrange("p a b -> p (a b)"), wtile("peT", 0), [R_wbf], [R_const], "d_c_peT")
        for s_ in range(2):
            for pc in range(2):
                wt, rw = load_wA("w1c", (s_ * 2 + pc) * 2)
                mm(banks[0][:, 0:1], [(wt[0:64, l * 128:(l + 1) * 128], c_peT[0:64, s_, l:l + 1]) for l in range(32)],
                   [rw, R_const], [Rb[0]])
                tt("dve", cbias[:, s_ * 2 + pc:s_ * 2 + pc + 1], banks[0][:, 0:1], lnp("b1T", s_ * 2 + pc), ALU.add,
                   [Rb[0], R_pc], [R_const])
        P.barrier(with_sp=True)

        def layer_norm(idx, sq, nmean, msq, tbuf):
            for dc in range(8):
                tcopy("dve", hb[:, dc, :], h32[:, dc, :], [R_h[dc]], [R_hb[dc]])
                act(sq[0][:, dc, :], h32[:, dc, :], AF.Square, [R_h[dc]], [sq[1]])
            mm(banks[6][:], [(c_ones[:], hb[:, dc, :]) for dc in range(8)], R_hb + [R_const], [Rb[6]])
            mm(banks[7][:], [(c_ones[:], sq[0][:, dc, :]) for dc in range(8)], [sq[1], R_const], [Rb[7]])
            ts("dve", nmean[0][:], banks[6][:], -1.0 / D, ALU.mult, [Rb[6]], [nmean[1]])
            tt("dve", msq[0][:], nmean[0][:], nmean[0][:], ALU.mult, [nmean[1]], [msq[1]])
            stt("dve", msq[0][:], banks[7][:], 1.0 / D, msq[0][:], ALU.mult, ALU.subtract, [Rb[7], msq[1]], [msq[1]])
            act(msq[0][:], msq[0][:], AF.Sqrt, [msq[1], R_pc], [msq[1]], bias=lnp("epsln", 0))
            P.op("dve", (lambda o, i: lambda e: e.reciprocal(out=o, in_=i))(msq[0][:], msq[0][:]), [msq[1]], [msq[1]])
            for dc in range(8):
                tb = tbuf[dc % 2]
                tt("dve", tb[0][:], h32[:, dc, :], nmean[0][:], ALU.add, [R_h[dc], nmean[1]], [tb[1]])
                tt("dve", tb[0][:], tb[0][:], msq[0][:], ALU.mult, [tb[1], msq[1]], [tb[1]])
                act(h32[:, dc, :], tb[0][:], AF.Identity, [tb[1], R_pc], [R_h[dc]],
                    bias=lnp("lnB", idx * 8 + dc), scale=lnp("lnG", idx * 8 + dc))
                act(hb[:, dc, :], tb[0][:], AF.Identity, [tb[1], R_pc], [R_hb[dc]],
                    bias=lnp("lnB", idx * 8 + dc), scale=lnp("lnG", idx * 8 + dc))

        def ln_scratch():
            sq = (sb([128, 8, CH], BF16, "sq"), Res())
            nmean = (sb([128, CH], F32, "nmean"), Res())
            msq = (sb([128, CH], F32, "msq"), Res())
            tbuf = [(sb([128, CH], F32, "tb"), Res()) for _ in range(2)]
            return sq, nmean, msq, tbuf

        def ffn(l, j):
            reset_local()
            g = sb([128, NFC, CH], BF16, "g"); R_g = [Res() for _ in range(NFC)]
            sa = [(sb([128, CH], F32, "sa"), Res()) for _ in range(2)]
            lsc = ln_scratch()
            cres = 0.5 / ALPHA
            if do_ffn:
                for gi in range(11):
                    wt, rw = load_wA("w13_%d_%d" % (l, j), gi)
                    for fi in range(2):
                        fc = gi * 2 + fi
                        pa, ra = banks[fc % 2], Rb[fc % 2]
                        pb_, rb_ = banks[2 + fc % 2], Rb[2 + fc % 2]
                        mm(pa[:], [(wt[:, kc * 256 + fi * 128:kc * 256 + fi * 128 + 128], hb[:, kc, :]) for kc in range(8)],
                           [rw] + R_hb, [ra])
                        mm(pb_[:], [(wt[:, 2048 + kc * 256 + fi * 128:2048 + kc * 256 + fi * 128 + 128], hb[:, kc, :])
                                    for kc in range(8)], [rw] + R_hb, [rb_])
                        s_t, s_r = sa[fc % 2]
                        act(s_t[:], pa[:], AF.Silu, [ra], [s_r])
                        tt("dve", g[:, fc, :], s_t[:], pb_[:], ALU.mult, [s_r, rb_], [R_g[fc]])
                for dg in range(8):
                    wt, rw = load_wB("w2_%d_%d" % (l, j), dg, NFC * 128)
                    py, ry = banks[4 + dg % 2], Rb[4 + dg % 2]
                    mm(py[:], [(wt[:, fc * 128:(fc + 1) * 128], g[:, fc, :]) for fc in range(NFC)], [rw] + R_g, [ry])
                    stt("dve", h32[:, dg, :], py[:], cres, h32[:, dg, :], ALU.mult, ALU.add, [ry, R_h[dg]], [R_h[dg]])
            layer_norm(l * 3 + (0 if j == 0 else 2), *lsc)
            P.barrier()

        def sgu(l):
            a = l
            reset_local()
            u = sb([128, 16, CH], BF16, "u"); R_u = [Res() for _ in range(16)]
            vg_off = ptr[0]
            vg = sb([128, 4, 2048], F32, "vg"); R_vg = [Res() for _ in range(4)]
            vln = [(sb([128, 2048], BF16, "vln"), Res()) for _ in range(2)]
            addt = sb([128, 16, 128], F32, "addt"); R_addt = Res()
            stats = sb([128, 4, 24], F32, "stats"); R_stats = Res()
            mv = sb([128, 4, 2], F32, "mv"); R_mv = Res()
            rstd = sb([128, 4], F32, "rstd"); R_rstd = Res()
            tmpm = [(sb([128, 128], F32, "tmpm"), Res()) for _ in range(2)]
            dma(WT[:].rearrange("p g t -> p (g t)"), wtile("wsT_%d" % a, 0), [R_wbf], [R_WT], "d_WT")
            for gq in range(8):
                tt("dve", WT[:, gq, :], WT[:, gq, :], c_tril[:], ALU.mult, [R_WT, R_const], [R_WT])
            wtv = WT[:].rearrange("p g t -> p (g t)")
            mm(banks[6][:], [(c_ones[:], wtv[:, 0:512])], [R_WT, R_const], [Rb[6]])
            mm(banks[7][:], [(c_ones[:], wtv[:, 512:1024])], [R_WT, R_const], [Rb[7]])
            for cc in range(16):
                gq = cc // 2
                bk = banks[6 + gq // 4]
                stt("dve", addt[:, cc, :], bk[:, (gq % 4) * 128:(gq % 4) * 128 + 128], lnp("sgB", a * 16 + cc),
                    bc_t[:, 512 + (a * 8 + gq) * 128:512 + (a * 8 + gq) * 128 + 128], ALU.mult, ALU.add,
                    [Rb[6 + gq // 4], R_pc, R_bc], [R_addt])
            for gi in range(8):
                wt, rw = load_wA("win_%d" % a, gi, 2048)
                for fi in range(2):
                    cc = gi * 2 + fi
                    pu, ru = banks[cc % 2], Rb[cc % 2]
                    mm(pu[:], [(wt[:, kc * 256 + fi * 128:kc * 256 + fi * 128 + 128], hb[:, kc, :]) for kc in range(8)],
                       [rw] + R_hb, [ru])
                    act(u[:, cc, :], pu[:], AF.Gelu, [ru], [R_u[cc]])
            k = 0
            for gi in range(8):
                wt, rw = load_wA("win_%d" % a, 8 + gi, 2048)
                for t4 in range(4):
                    pv, rv = banks[2 + k % 2], Rb[2 + k % 2]
                    k += 1
                    mm(pv[:, 0:256], [(hb[:, kc, t4 * 128:(t4 + 1) * 128], wt[:, kc * 256:(kc + 1) * 256]) for kc in range(8)],
                       [rw] + R_hb, [rv])
                    act(vg[:, t4, gi * 256:(gi + 1) * 256], pv[:, 0:256], AF.Gelu, [rv], [R_vg[t4]])
            for t4 in range(4):
                for q in range(4):
                    P.op("dve", (lambda o, i: lambda e: e.bn_stats(out=o, in_=i))(stats[:, t4, q * 6:(q + 1) * 6],
                                                                               vg[:, t4, q * 512:(q + 1) * 512]),
                         [R_vg[t4]], [R_stats])
                P.op("dve", (lambda o, i: lambda e: e.bn_aggr(out=o, in_=i))(mv[:, t4, :], stats[:, t4, :]),
                     [R_stats], [R_mv])
            act(rstd[:, 0:4], mv[:, :, 1], AF.Sqrt, [R_mv, R_pc], [R_rstd], bias=lnp("epssg", 0))
            P.op("dve", (lambda o, i: lambda e: e.reciprocal(out=o, in_=i))(rstd[:, 0:4], rstd[:, 0:4]), [R_rstd], [R_rstd])
            for t4 in range(4):
                vl, rvl = vln[t4 % 2]
                ts("dve", vl[:], vg[:, t4, :], mv[:, t4, 0:1], ALU.subtract, [R_vg[t4], R_mv, R_rstd], [rvl],
                   s2=rstd[:, t4:t4 + 1], op1=ALU.mult)
                for cq in range(4):
                    pm, rm = banks[4 + cq % 2], Rb[4 + cq % 2]
                    mm_multi([(pm[:, ci * 128:(ci + 1) * 128], vl[:, (cq * 4 + ci) * 128:(cq * 4 + ci + 1) * 128],
                               WT[:, (cq * 4 + ci) // 2, :], True, True, False) for ci in range(4)],
                             [rvl, R_WT], [rm])
                    for ci in range(4):
                        cc = cq * 4 + ci
                        tm, rtm = tmpm[ci % 2]
                        stt("dve", tm[:], pm[:, ci * 128:(ci + 1) * 128], lnp("sgG", a * 16 + cc), addt[:, cc, :],
                            ALU.mult, ALU.add, [rm, R_pc, R_addt], [rtm])
                        tt("dve", u[:, cc, t4 * 128:(t4 + 1) * 128], tm[:], u[:, cc, t4 * 128:(t4 + 1) * 128], ALU.mult,
                           [rtm, R_u[cc]], [R_u[cc]])
            for dg in range(8):
                wt, rw = load_wB("wout_%d" % a, dg, 16 * 128)
                py, ry = banks[dg % 2], Rb[dg % 2]
                mm(py[:], [(wt[:, cc * 128:(cc + 1) * 128], u[:, cc, :]) for cc in range(16)], [rw] + R_u, [ry])
                stt("dve", h32[:, dg, :], py[:], 1.0 / ALPHA, h32[:, dg, :], ALU.mult, ALU.add, [ry, R_h[dg]], [R_h[dg]])
            P.barrier()
            ptr[0] = vg_off
            lsc = ln_scratch()
            layer_norm(l * 3 + 1, *lsc)
            P.barrier()

        def kv_prep(c):
            reset_local()
            T0 = c * CH
            kv0 = sb([128, 4, 528], BF16, "kv0"); R_kv0 = Res()
            hid = sb([128, 2, 2, 4, 32], BF16, "hid"); R_hid = Res()
            vct = sb([128, 4, 64], BF16, "vct"); R_vct = Res()
            for gi in range(4):
                wt, rw = load_wA("kk", gi, 2048)
                for fi in range(2):
                    fch = gi * 2 + fi
                    pk, rk = banks[fch % 2], Rb[fch % 2]
                    mm(pk[:], [(wt[:, kc * 256 + fi * 128:kc * 256 + fi * 128 + 128], hb[:, kc, :]) for kc in range(8)],
                       [rw] + R_hb, [rk])
                    if fch % 2 == 0:
                        tcopy("dve", kTs[:, fch, T0:T0 + CH], pk[:], [rk], [R_kTs])
                    else:
                        act(kTs[:, fch, T0:T0 + CH], pk[:], AF.Copy, [rk], [R_kTs])
            wt, rw = load_wA("kvv", 0)
            for t4 in range(4):
                pv, rv = banks[2 + t4 % 2], Rb[2 + t4 % 2]
                mm(pv[:], [(hb[:, kc, t4 * 128:(t4 + 1) * 128], wt[:, kc * 512:(kc + 1) * 512]) for kc in range(8)],
                   [rw] + R_hb, [rv])
                tcopy("dve", Vaug[:, 4 * c + t4, :, 0:64], pv[:].rearrange("p (a d) -> p a d", d=64), [rv], [R_V])
            if stage < 0.1:
                P.barrier()
                return
            tcopy("dve", kv0[:, :, 0:16], carry[:], [R_carry], [R_kv0])
            wt, rw = load_wA("kvc", 0)
            for fch in range(4):
                pc_, rc_ = banks[4 + fch % 2], Rb[4 + fch % 2]
                mm(pc_[:], [(wt[:, kc * 512 + fch * 128:kc * 512 + fch * 128 + 128], hb[:, kc, :]) for kc in range(8)],
                   [rw] + R_hb, [rc_])
                act(kv0[:, fch, 16:528], pc_[:], AF.Copy, [rc_], [R_kv0])
            tcopy("dve", carry[:], kv0[:, :, 512:528], [R_kv0], [R_carry])
            if stage < 0.3:
                P.barrier()
                return
            for s_ in range(2):
                for pc in range(2):
                    ph, rh = banks[6 + (s_ * 2 + pc) % 2], Rb[6 + (s_ * 2 + pc) % 2]
                    for half in range(2):
                        wt, rw = load_wA("w1c", (s_ * 2 + pc) * 2 + half)
                        for gq in (half, half + 2):
                            fch = s_ * 2 + gq // 2
                            mm(ph[:, gq * 32:(gq + 1) * 32],
                               [(wt[:, l * 128:(l + 1) * 128], kv0[:, fch, l:l + 497:16]) for l in range(32)],
                               [rw, R_kv0], [rh])
                    if s_ == 0:
                        act(hid[:, s_, pc, :, :].rearrange("p g n -> p (g n)"), ph[:, 0:128], AF.Gelu, [rh, R_const], [R_hid],
                            bias=cbias[:, s_ * 2 + pc:s_ * 2 + pc + 1])
                    else:
                        act(hidv[:, pc, :, 32 * c:32 * c + 32], ph[:, 0:128].rearrange("p (g n) -> p g n", n=32), AF.Gelu,
                            [rh, R_const], [R_hidv], bias=cbias[:, s_ * 2 + pc:s_ * 2 + pc + 1])
            if stage < 0.5:
                P.barrier()
                return
            pk, rk = banks[0], Rb[0]
            for gq in range(4):
                mm(pk[:, gq * 32:(gq + 1) * 32], [(c_w2ck[:, pc, :], hid[:, 0, pc, gq, :]) for pc in range(2)],
                   [R_hid, R_const], [rk])
            tcopy("dve", kcT[:, :, 32 * c:32 * c + 32], pk[:, 0:128].rearrange("p (g n) -> p g n", n=32), [rk], [R_kc])
            if stage < 0.7:
                P.barrier()
                return
            NV = 32 * (c + 1)
            pv, rv = banks[1], Rb[1]
            for gq in range(4):
                mm(pv[:, gq * 64:(gq + 1) * 64], [(hidv[:, pc, gq, :], c_w2cv[:, pc, :]) for pc in range(2)],
                   [R_hidv, R_const], [rv])
            if stage != 0.8:
                if stage == 0.9:
                    tcopy("dve", kv0[:, 0, 0:256], pv[:, 0:256], [rv], [R_kv0])
                else:
                    act(vcmp[:].rearrange("p g d -> p (g d)"), pv[:, 0:256], AF.Copy, [rv], [R_vc])
            P.barrier()

        def nsa(l, c):
            bl = l - 2
            reset_local()
            T0 = c * CH
            NV = 32 * (c + 1)
            q_off = ptr[0]
            qT = sb([128, 8, 2, CH], BF16, "qT"); R_q = [Res() for _ in range(8)]
            gates = sb([128, 4, 48], F32, "gates"); R_gates = Res()
            snT = sb([128, 4, CH], BF16, "snT"); R_sn = [Res() for _ in range(4)]
            Pe = [(sb([128, 128], F32, "Pe"), Res()) for _ in range(2)]
            Pn = [(sb([128, 128], BF16, "Pn"), Res()) for _ in range(2)]
            PT = [(sb([128, 128], BF16, "PT"), Res()) for _ in range(2)]
            ET = [(sb([128, CH], BF16, "ET"), Res()) for _ in range(3)]
            small = sb([128, 64], F32, "small"); R_small = Res()
            imp_t = sb([128, 32], F32, "imp"); R_imp = Res()
            top8 = sb([128, 8], F32, "top8")
            snq = sb([128, 32], BF16, "snq"); R_snq = Res()
            o_acc = sb([128, 4, D], F32, "oacc"); R_oacc = [Res() for _ in range(4)]
            o_bf = sb([128, 4, D], BF16, "obf"); R_obf = [Res() for _ in range(4)]
            oTt = sb([128, 8, CH], BF16, "oT"); R_oT = [Res() for _ in range(8)]
            rdc = sb([128, 2, 8], F32, "rdc"); R_rdc = [Res(), Res()]
            for b_ in range(2):
                memset("dve", Pn[b_][0][:], 0.0, [Pn[b_][1]])
            for hc_ in range(8):
                memset("dve", qT[:, hc_, :, :], 0.0, [R_q[hc_]])
            if stage < 2:
                P.barrier()
                ptr[0] = q_off
                layer_norm(l * 3 + 1, *ln_scratch())
                P.barrier()
                return
            for gi in range(4):
                wt, rw = load_wA("wq_%d" % bl, gi, 2048)
                for fi in range(2):
                    hc = gi * 2 + fi
                    pq, rq = banks[hc % 2], Rb[hc % 2]
                    mm(pq[:], [(wt[:, kc * 256 + fi * 128:kc * 256 + fi * 128 + 128], hb[:, kc, :]) for kc in range(8)],
                       [rw] + R_hb, [rq])
                    act(qT[0:64, hc, 0, :], pq[0:64, :], AF.Copy, [rq], [R_q[hc]], scale=0.125)
                    act(qT[64:128, hc, 1, :], pq[64:128, :], AF.Copy, [rq], [R_q[hc]], scale=0.125)
            dma(wgt[:].rearrange("p a b -> p (a b)"), wtile("wg_%d" % bl, 0), [R_wbf], [R_wgt], "d_wgt")
            pg, rg = banks[2], Rb[2]
            for t4 in range(4):
                mm(pg[:, t4 * 48:(t4 + 1) * 48], [(hb[:, kc, t4 * 128:(t4 + 1) * 128], wgt[:, kc, :]) for kc in range(8)],
                   [R_wgt] + R_hb, [rg])
            act(gates[:].rearrange("p a b -> p (a b)"), pg[:, 0:192], AF.Sigmoid, [rg], [R_gates])
            if stage < 3:
                P.barrier()
                ptr[0] = q_off
                layer_norm(l * 3 + 1, *ln_scratch())
                P.barrier()
                return
            k = 0
            for qt in range(4):
                QT = 4 * c + qt
                xoff = 120 - 8 * QT
                ioff = 30 - 2 * QT
                for gq in range(4):
                    pimp, rimp = banks[3], Rb[3]
                    for r in range(4):
                        h = 4 * gq + r
                        hc = h // 2
                        r0 = (h % 2) * 64
                        psc, rsc = banks[k % 2], Rb[k % 2]
                        pe_t, pe_r = Pe[k % 2]
                        pn_t, pn_r = Pn[k % 2]
                        pt_t, pt_r = PT[k % 2]
                        k += 1
                        mm_multi([(psc[:, 0:NV], qT[:, hc, h % 2, qt * 128:(qt + 1) * 128], kcT[:, gq, 0:NV],
                                   True, False, False),
                                  (psc[:, 0:NV], c_ident[:], MC[:, h, xoff:xoff + NV], False, True, False)],
                                 [R_q[hc], R_kc, R_const, R_mast], [rsc])
                        sm = small[:, (k % 2) * 8:(k % 2) * 8 + 8]
                        P.op("dve", (lambda o, i: lambda e: e.reduce_max(out=o, in_=i, axis=AX.X))(sm[:, 0:1], psc[:, 1:NV]),
                             [rsc], [R_small])
                        ts("dve", sm[:, 1:2], sm[:, 0:1], -1.0, ALU.mult, [R_small], [R_small])
                        act(pe_t[:, 1:NV], psc[:, 1:NV], AF.Exp, [rsc, R_small], [pe_r, R_small], bias=sm[:, 1:2],
                            accum=sm[:, 2:3])
                        P.op("dve", (lambda o, i: lambda e: e.reciprocal(out=o, in_=i))(sm[:, 3:4], sm[:, 2:3]),
                             [R_small], [R_small])
                        if QT == 0:
                            tt("dve", sm[:, 3:4], sm[:, 3:4], lnp("rv0", 0), ALU.mult, [R_small, R_pc], [R_small])
                        ts("dve", pn_t[:, 1:NV], pe_t[:, 1:NV], sm[:, 3:4], ALU.mult, [pe_r, R_small], [pn_r])
                        ptp = banks[2][:, 256:320].bitcast(BF16)
                        P.op("pe", (lambda o, i: lambda e: e.transpose(o, i, c_ident[:]))(ptp[0:NV, :], pn_t[:, 0:NV]),
                             [pn_r, R_const], [Rb[2]])
                        tcopy("dve", pt_t[0:NV, :], ptp[0:NV, :], [Rb[2]], [pt_r])
                        poc = banks[2][:, 320 + (r % 2) * 64:320 + (r % 2) * 64 + 64]
                        mm(poc, [(pt_t[0:NV, :], vcmp[0:NV, gq, :])], [pt_r, R_vc], [Rb[2]])
                        mm(pimp[:, 0:32], [(pt_t[0:NV, :], c_ov[0:NV, :])], [pt_r, R_const], [rimp],
                           start=(r == 0), stop=(r == 3))
                        ts("dve", o_acc[:, qt, h * 64:(h + 1) * 64], poc, gates[:, qt, h * 3:h * 3 + 1], ALU.mult,
                           [Rb[2], R_gates], [R_oacc[qt]])
                    o_vm, _ = pcols["VM"]
                    o_fb, _ = pcols["FB"]
                    tt("dve", imp_t[:], pimp[:, 0:32], pc_t[:, o_vm + ioff:o_vm + ioff + 32], ALU.mult, [rimp, R_pc], [R_imp])
                    tt("dve", imp_t[:], imp_t[:], pc_t[:, o_fb + ioff:o_fb + ioff + 32], ALU.add, [R_imp, R_pc], [R_imp])
                    memset("dve", imp_t[:, 0:1], BIG, [R_imp])
                    P.op("dve", (lambda o, i: lambda e: e.max(out=o, in_=i))(top8[:], imp_t[:]), [R_imp], [R_small])
                    ts("dve", imp_t[:], imp_t[:], top8[:, 7:8], ALU.is_lt, [R_imp, R_small], [R_imp])
                    ts("dve", snq[:], imp_t[:], NEGM, ALU.mult, [R_imp], [R_snq])
                    pst = banks[2][:, 448:512].bitcast(BF16)
                    P.op("pe", (lambda o, i: lambda e: e.transpose(o, i, c_ident[:]))(pst[0:32, :], snq[:]),
                         [R_snq, R_const], [Rb[2]])
                    tcopy("dve", snT[0:32, gq, qt * 128:(qt + 1) * 128], pst[0:32, :], [Rb[2]], [R_sn[gq]])
            if stage < 4 and stage not in (3.2, 3.4, 3.6):
                P.barrier()
                ptr[0] = q_off
                layer_norm(l * 3 + 1, *ln_scratch())
                P.barrier()
                return
            o_c31 = 496
            kst = 0
            ke = 0
            for h in range(16):
                gq = h // 4
                hc = h // 2
                r0 = (h % 2) * 64
                for br in range(2):
                    if (stage == 3.2 and br == 1) or (stage == 3.4 and br == 0):
                        continue
                    pO, rO = banks[6 + br], Rb[6 + br]
                    P.op("dve", (lambda o: lambda e: e.memset(o, 0.0))(pO[:, 0:260]), [], [rO])
                    kt_lo = 0 if br == 0 else max(0, 4 * c - 4)
                    for kt in range(kt_lo, 4 * c + 4):
                        bs = [b for b in range(4) if (4 * c + b - kt) >= 0 and (br == 0 or (4 * c + b - kt) <= 4)]
                        if not bs:
                            continue
                        b_lo, b_hi = bs[0], bs[-1]
                        c0, c1 = b_lo * 128, (b_hi + 1) * 128
                        pst_, rst_ = banks[4 + kst % 2], Rb[4 + kst % 2]
                        kst += 1
                        items = [(pst_[:, c0:c1], kTs[:, br * 4 + gq, kt * 128:(kt + 1) * 128],
                                  qT[:, hc, h % 2, c0:c1], True, False, False)]
                        reads = [R_kTs, R_q[hc], R_const, R_mast]
                        if br == 0:
                            items.append((pst_[:, c0:c1], c_emast[0:32, kt * 128:(kt + 1) * 128], snT[0:32, gq, c0:c1],
                                          False, False, False))
                            reads.append(R_sn[gq])
                        for b in bs:
                            d = 4 * c + b - kt
                            if d == 0:
                                items.append((pst_[:, b * 128:(b + 1) * 128], c_ident[:], MT[:, h, 0:128], False, False, False))
                            elif d == 1:
                                items.append((pst_[:, b * 128:(b + 1) * 128], c_ident[:], MT[:, h, 128:256], False, False, False))
                            elif d == 4 and br == 1 and stage != 3.6:
                                items.append((pst_[:, b * 128:(b + 1) * 128], c_ident[:], c_wmask[:], False, False, False))
                        it = items[-1]
                        items[-1] = (it[0], it[1], it[2], it[3], True, it[5])
                        mm_multi(items, reads, [rst_])
                        et_t, et_r = ET[ke % 3]
                        ke += 1
                        act(et_t[:, c0:c1], pst_[:, c0:c1], AF.Exp, [rst_, R_bc], [et_r], bias=bc_t[:, o_c31 + h:o_c31 + h + 1])
                        mm_multi([(pO[:, b * 65:(b + 1) * 65], et_t[:, b * 128:(b + 1) * 128], Vaug[:, kt, br * 4 + gq, :],
                                   False, False, True) for b in bs], [et_r, R_V], [rO])
                    rd_t = rdc[:, br, :]
                    dview = pO[:, 0:260].rearrange("p (b x) -> p b x", x=65)
                    P.op("dve", (lambda o, i: lambda e: e.reciprocal(out=o, in_=i))(rd_t[:, 0:4], dview[:, :, 64]),
                         [rO], [R_rdc[br]])
                    tt("dve", rd_t[:, 4:8], rd_t[:, 0:4], gates[:, :, h * 3 + 1 + br], ALU.mult, [R_rdc[br], R_gates], [R_rdc[br]])
                    for b in range(4):
                        stt("dve", o_acc[:, b, h * 64:(h + 1) * 64], pO[:, b * 65:b * 65 + 64], rd_t[:, 4 + b:5 + b],
                            o_acc[:, b, h * 64:(h + 1) * 64], ALU.mult, ALU.add, [rO, R_rdc[br], R_oacc[b]], [R_oacc[b]])
            if stage < 5 and stage not in (3.2, 3.4, 3.6):
                P.barrier()
                ptr[0] = q_off
                layer_norm(l * 3 + 1, *ln_scratch())
                P.barrier()
                return
            for b in range(4):
                if b % 2 == 0:
                    tcopy("dve", o_bf[:, b, :], o_acc[:, b, :], [R_oacc[b]], [R_obf[b]])
                else:
                    act(o_bf[:, b, :], o_acc[:, b, :], AF.Copy, [R_oacc[b]], [R_obf[b]])
            for hc in range(8):
                ptr_, rtr_ = banks[hc % 2], Rb[hc % 2]
                ptv = ptr_[:, 0:256].bitcast(BF16)
                for b in range(4):
                    P.op("pe", (lambda o, i: lambda e: e.transpose(o, i, c_ident[:]))(
                        ptv[:, b * 128:(b + 1) * 128], o_bf[:, b, hc * 128:(hc + 1) * 128]), [R_obf[b], R_const], [rtr_])
                if hc % 2 == 0:
                    tcopy("dve", oTt[:, hc, :], ptv[:, 0:512], [rtr_], [R_oT[hc]])
                else:
                    act(oTt[:, hc, :], ptv[:, 0:512], AF.Copy, [rtr_], [R_oT[hc]])
            for gi in range(4):
                wt, rw = load_wA("wo_%d" % bl, gi, 2048)
                for fi in range(2):
                    dc = gi * 2 + fi
                    py, ry = banks[2 + dc % 2], Rb[2 + dc % 2]
                    mm(py[:], [(wt[:, kc * 256 + fi * 128:kc * 256 + fi * 128 + 128], oTt[:, kc, :]) for kc in range(8)],
                       [rw] + R_oT, [ry])
                    stt("dve", h32[:, dc, :], py[:], 1.0 / ALPHA, h32[:, dc, :], ALU.mult, ALU.add, [ry, R_h[dc]], [R_h[dc]])
            P.barrier()
            ptr[0] = q_off
            layer_norm(l * 3 + 1, *ln_scratch())
            P.barrier()

        R_out = Res()
        for s_ in range(nseq):
            for c in range(nchunk):
                T0 = c * CH
                if c == 0:
                    memset("pool", carry[:], 0.0, [R_carry])
                src = xT.ap()[s_].rearrange("(dc p) t -> p dc t", p=128)[:, :, T0:T0 + CH]
                dma(h32[:], src, [], R_h, "d_x")
                for dc in range(8):
                    tcopy("dve", hb[:, dc, :], h32[:, dc, :], [R_h[dc]], [R_hb[dc]])
                for l in layers:
                    if l == 2 and do_mixer:
                        kv_prep(c)
                    ffn(l, 0)
                    if do_mixer:
                        if l < 2:
                            sgu(l)
                        else:
                            nsa(l, c)
                    else:
                        reset_local()
                        lsc = ln_scratch()
                        layer_norm(l * 3 + 1, *lsc)
                        P.barrier()
                    ffn(l, 1)
                dst = oT.ap()[s_].rearrange("(dc p) t -> p dc t", p=128)[:, :, T0:T0 + CH]
                dma(dst, h32[:], R_h, [R_out], "d_out")
        P.final_wait("sp")
        P.replay(st)
    return nc


_CFG = {}


def kernel(**inputs):
    inp = {k: np.asarray(v) for k, v in inputs.items()}
    x = inp["x"].astype(np.float32, copy=False)
    wc = _wlayout(inp)
    wflat = wc.finish()
    pcat, pcols, bcat, ohT, ohC = _small_params(inp)
    nc = build_nc(wc.off, wflat.size, pcols, pcat.shape[1], _CFG)
    in_maps = []
    ncores = _CFG.get("ncores", NCORES)
    for core in range(ncores):
        xs = np.ascontiguousarray(x[core * BPC:(core + 1) * BPC].transpose(0, 2, 1))
        in_maps.append({"xT": xs, "wcat": wflat, "pcat": pcat, "bcat": bcat, "ohT": ohT, "ohC": ohC})
    res = run_bass_kernel_spmd(nc, in_maps, core_ids=list(range(ncores)))
    out = np.zeros((NCORES * BPC, S, D), np.float32)
    for core in range(ncores):
        o = np.asarray(res.results[core]["oT"])
        out[core * BPC:(core + 1) * BPC] = o.transpose(0, 2, 1)
    return out
```
